# Optimizing a Trainium2 kernel written in Bass

```python
import jax, jax.numpy as jnp
from jax import lax
import numpy as np


D_MODEL = 1024
BATCH = 8
SEQ = 8192
DEPTH = 4

HEAD_DIM = 64
SWA_Q_HEADS = 6
SWA_KV_HEADS = 2
SWA_WINDOW = 128
SB_HEADS = 4
RET_HEADS = 6
RET_CHUNK = 128
BLOCK = 128
ROPE_THETA = 10000.0
D_FF = 4 * D_MODEL
NORM_EPS = 1e-6

SWA_Q_W = SWA_Q_HEADS * HEAD_DIM
SWA_KV_W = SWA_KV_HEADS * HEAD_DIM
SB_W = SB_HEADS * HEAD_DIM
RET_W = RET_HEADS * HEAD_DIM
MIX_W = SWA_Q_W + SB_W + RET_W
IN_W = SWA_Q_W + 2 * SWA_KV_W + 3 * SB_W + 4 * RET_W
IN_SPLITS = np.cumsum([SWA_Q_W, SWA_KV_W, SWA_KV_W, SB_W, SB_W, SB_W, RET_W, RET_W, RET_W]).tolist()

kernel_name = 'hybrid_swa_stickbreak_retention_block'


def rms_norm(x, gain):
    xf = x.astype(jnp.float32)
    y = xf * lax.rsqrt(jnp.mean(xf * xf, axis=-1, keepdims=True) + NORM_EPS)
    return (y * gain.astype(jnp.float32)).astype(x.dtype)


def rope_tables(positions):
    inv_freq = ROPE_THETA ** (-jnp.arange(0, HEAD_DIM, 2, dtype=jnp.float32) / HEAD_DIM)
    ang = positions.astype(jnp.float32)[:, None] * inv_freq[None, :]
    return jnp.cos(ang), jnp.sin(ang)


def apply_rope(x, cos, sin):
    x1, x2 = jnp.split(x.astype(jnp.float32), 2, axis=-1)
    c = cos[None, :, None, :]
    s = sin[None, :, None, :]
    return jnp.concatenate([x1 * c - x2 * s, x1 * s + x2 * c], axis=-1).astype(x.dtype)


def swa_sink_attention(q, k, v, sinks):
    b, s, hq, hd = q.shape
    nb = s // BLOCK
    g = hq // SWA_KV_HEADS
    qb = q.reshape(b, nb, BLOCK, SWA_KV_HEADS, g, hd)

    def band(t):
        tb = t.reshape(b, nb, BLOCK, SWA_KV_HEADS, hd)
        prev = jnp.pad(tb, ((0, 0), (1, 0), (0, 0), (0, 0), (0, 0)))[:, :-1]
        return jnp.concatenate([prev, tb], axis=2)

    kb, vb = band(k), band(v)
    scores = jnp.einsum('bnqhgd,bnkhd->bnhgqk', qb, kb,
                        preferred_element_type=jnp.float32) * (hd ** -0.5)
    qi = jnp.arange(BLOCK)[:, None] + BLOCK
    ki = jnp.arange(2 * BLOCK)[None, :]
    rel = qi - ki
    in_window = (rel >= 0) & (rel < SWA_WINDOW)
    key_abs = jnp.arange(nb)[:, None, None] * BLOCK + ki[None] - BLOCK
    valid = in_window[None] & (key_abs >= 0)
    scores = jnp.where(valid[None, :, None, None], scores, -jnp.inf)
    sink = sinks.astype(jnp.float32).reshape(SWA_KV_HEADS, g)[None, None, :, :, None, None]
    sink = jnp.broadcast_to(sink, scores.shape[:-1] + (1,))
    probs = jax.nn.softmax(jnp.concatenate([scores, sink], axis=-1), axis=-1)[..., :-1]
    out = jnp.einsum('bnhgqk,bnkhd->bnqhgd', probs.astype(v.dtype), vb)
    return out.reshape(b, s, hq * hd)


def stick_breaking_attention(q, k, v):
    b, s, h, hd = q.shape
    nb = s // BLOCK
    qb = jnp.moveaxis(q.reshape(b, nb, BLOCK, h, hd), 1, 0)
    kpos = jnp.arange(s)

    def one_block(args):
        qblk, i = args
        z = jnp.einsum('bqhd,bkhd->bhqk', qblk, k,
                       preferred_element_type=jnp.float32) * (hd ** -0.5)
        qpos = i * BLOCK + jnp.arange(BLOCK)
        strict = kpos[None, :] < qpos[:, None]
        log_beta = jax.nn.log_sigmoid(z)
        log_1m = jnp.where(strict, jax.nn.log_sigmoid(-z), 0.0)
        tail = lax.cumsum(log_1m, axis=3, reverse=True) - log_1m
        w = jnp.where(strict, jnp.exp(log_beta + tail), 0.0)
        return jnp.einsum('bhqk,bkhd->bqhd', w.astype(v.dtype), v)

    out = lax.map(one_block, (qb, jnp.arange(nb)))
    return jnp.moveaxis(out, 0, 1).reshape(b, s, h * hd)


def retention(q, k, v, gate, gn_gain):
    b, s, h, hd = q.shape
    nc = s // RET_CHUNK
    log_gamma = jnp.log1p(-(2.0 ** (-5.0 - jnp.arange(h, dtype=jnp.float32))))
    f = lambda t: t.astype(jnp.float32).reshape(b, nc, RET_CHUNK, h, hd)
    qc, kc, vc = f(q), f(k) * (hd ** -0.5), f(v)
    pos = jnp.arange(RET_CHUNK, dtype=jnp.float32)
    rel = pos[:, None] - pos[None, :]
    decay = jnp.where(rel[None] >= 0,
                      jnp.exp(jnp.maximum(rel, 0.0)[None] * log_gamma[:, None, None]), 0.0)
    intra = jnp.einsum('bnqhd,bnkhd->bnhqk', qc, kc) * decay[None, None]
    o_intra = jnp.einsum('bnhqk,bnkhd->bnqhd', intra, vc)
    k_dec = jnp.exp((RET_CHUNK - 1 - pos)[:, None] * log_gamma[None, :])
    kv = jnp.einsum('bnkhd,bnkhe->nbhde', kc * k_dec[None, None, :, :, None], vc)
    chunk_decay = jnp.exp(RET_CHUNK * log_gamma)[None, :, None, None]

    def step(state, kv_n):
        return chunk_decay * state + kv_n, state

    _, prev_states = lax.scan(step, jnp.zeros((b, h, hd, hd), jnp.float32), kv)
    q_dec = jnp.exp((pos + 1.0)[:, None] * log_gamma[None, :])
    o_cross = jnp.einsum('bnqhd,nbhde->bnqhe', qc * q_dec[None, None, :, :, None], prev_states)
    o = (o_intra + o_cross).reshape(b, s, h, hd)
    mu = jnp.mean(o, axis=-1, keepdims=True)
    var = jnp.mean(jnp.square(o - mu), axis=-1, keepdims=True)
    o = (o - mu) * lax.rsqrt(var + NORM_EPS) * gn_gain.astype(jnp.float32).reshape(h, hd)
    o = jax.nn.silu(gate.astype(jnp.float32)) * o
    return o.reshape(b, s, h * hd).astype(q.dtype)


def hybrid_layer(x, cos, sin, w_in, w_out, sinks, branch_gain, w_up, w_down,
                 g_mix_pre, g_mix_post, g_mlp_pre, g_mlp_post):
    b, s, _ = x.shape
    heads = lambda t, n: t.reshape(b, s, n, HEAD_DIM)
    hn = rms_norm(x, g_mix_pre)
    proj = jnp.einsum('bsd,de->bse', hn, w_in)
    qa, ka, va, qb, kb, vb, qc, kc, vc, gc = jnp.split(proj, IN_SPLITS, axis=-1)
    qa = apply_rope(heads(qa, SWA_Q_HEADS), cos, sin)
    ka = apply_rope(heads(ka, SWA_KV_HEADS), cos, sin)
    out_a = swa_sink_attention(qa, ka, heads(va, SWA_KV_HEADS), sinks)
    out_b = stick_breaking_attention(heads(qb, SB_HEADS), heads(kb, SB_HEADS), heads(vb, SB_HEADS))
    qc = apply_rope(heads(qc, RET_HEADS), cos, sin)
    kc = apply_rope(heads(kc, RET_HEADS), cos, sin)
    ga = branch_gain[:SWA_Q_W]
    gb = branch_gain[SWA_Q_W:SWA_Q_W + SB_W]
    gcn = branch_gain[SWA_Q_W + SB_W:]
    out_c = retention(qc, kc, heads(vc, RET_HEADS), heads(gc, RET_HEADS), gcn)
    mixed = jnp.concatenate([rms_norm(out_a, ga), rms_norm(out_b, gb), out_c], axis=-1)
    y = jnp.einsum('bse,ed->bsd', mixed, w_out)
    x = x + rms_norm(y, g_mix_post)
    hm = jnp.einsum('bsd,df->bsf', rms_norm(x, g_mlp_pre), w_up)
    hm = jnp.square(jax.nn.relu(hm))
    y = jnp.einsum('bsf,fd->bsd', hm, w_down)
    return x + rms_norm(y, g_mlp_post)


def setup_inputs(seed: int = 0) -> dict:
    key = jax.random.key(seed)
    ks = jax.random.split(key, 12)
    nrm = lambda k, shape, scale: jax.random.normal(k, shape, jnp.float32) * scale
    gain = lambda k, shape: 1.0 + 0.05 * jax.random.normal(k, shape, jnp.float32)
    return {
        'x': nrm(ks[0], (BATCH, SEQ, D_MODEL), 1.0),
        'positions': jnp.arange(SEQ, dtype=jnp.int32),
        'w_in': nrm(ks[1], (DEPTH, D_MODEL, IN_W), D_MODEL ** -0.5),
        'w_out': nrm(ks[2], (DEPTH, MIX_W, D_MODEL), MIX_W ** -0.5),
        'sinks': nrm(ks[3], (DEPTH, SWA_Q_HEADS), 0.5),
        'branch_gain': gain(ks[4], (DEPTH, MIX_W)),
        'w_up': nrm(ks[5], (DEPTH, D_MODEL, D_FF), D_MODEL ** -0.5),
        'w_down': nrm(ks[6], (DEPTH, D_FF, D_MODEL), D_FF ** -0.5),
        'norm_mix_pre': gain(ks[7], (DEPTH, D_MODEL)),
        'norm_mix_post': gain(ks[8], (DEPTH, D_MODEL)),
        'norm_mlp_pre': gain(ks[9], (DEPTH, D_MODEL)),
        'norm_mlp_post': gain(ks[10], (DEPTH, D_MODEL)),
    }


def reference(x, positions, w_in, w_out, sinks, branch_gain, w_up, w_down,
              norm_mix_pre, norm_mix_post, norm_mlp_pre, norm_mlp_post):
    cos, sin = rope_tables(positions)
    for layer in range(DEPTH):
        x = hybrid_layer(x, cos, sin, w_in[layer], w_out[layer], sinks[layer], branch_gain[layer],
                         w_up[layer], w_down[layer], norm_mix_pre[layer], norm_mix_post[layer],
                         norm_mlp_pre[layer], norm_mlp_post[layer])
    return x
```

```python
from contextlib import ExitStack
import math
import numpy as np
import concourse.bass as bass
import concourse.mybir as mybir
from concourse.bass_utils import run_bass_kernel_spmd

F32 = mybir.dt.float32
BF16 = mybir.dt.bfloat16
I32 = mybir.dt.int32
AF = mybir.ActivationFunctionType
ALU = mybir.AluOpType

D = 1024
S_FULL = 8192
DEPTH = 4
IN_W = 2944
DFF = 4096
EPS = 1e-6
EPOCH = 30000
DBG = {}
LOG_GAMMA = [math.log1p(-(2.0 ** (-5.0 - h))) for h in range(6)]

C_ID = 0
C_UI = 128
C_MS = 256
C_MC = 384
C_MP = 512
C_DD = 640
C_QD = 1408
C_KD = 1414
C_IF = 1420
C_ONE = 1452
NCON = 1456


def make_consts():
    c = np.zeros((128, NCON), np.float64)
    i = np.arange(128)
    c[:, C_ID:C_ID + 128] = np.eye(128)
    c[:, C_UI:C_UI + 128] = (i[:, None] >= i[None, :])
    c[:, C_MS:C_MS + 128] = (i[:, None] < i[None, :])
    c[:, C_MC:C_MC + 128] = (i[:, None] <= i[None, :])
    c[:, C_MP:C_MP + 128] = (i[:, None] > i[None, :])
    lg = np.array([np.log1p(-np.float32(2.0) ** np.float32(-5.0 - h)) for h in range(6)], np.float32).astype(np.float64)
    rel = i[None, :] - i[:, None]
    for h in range(6):
        dd = np.where(rel >= 0, np.exp(np.maximum(rel, 0) * lg[h]), 0.0) * 0.125
        c[:, C_DD + h * 128:C_DD + (h + 1) * 128] = dd
        c[:, C_QD + h] = np.exp((i + 1.0) * lg[h])
        c[:, C_KD + h] = np.exp((127.0 - i) * lg[h]) * 0.125
    invf = (np.float32(10000.0) ** (-(np.arange(0, 64, 2, dtype=np.float32)) / np.float32(64))).astype(np.float64)
    c[:, C_IF:C_IF + 32] = invf[None, :]
    c[:, C_ONE:C_ONE + 2] = 1.0
    return c.astype(np.float32)


class Sched:
    def __init__(self, nc, stack):
        self.nc = nc
        self.stack = stack
        self.eng_names = ["pe", "act", "dve", "pool", "sp"]
        self.ops = {e: [] for e in self.eng_names}
        self.count = {e: 0 for e in self.eng_names}
        self.prog_sems = {e: [] for e in self.eng_names}
        self.dma_sems = {}
        self.last_w = {}
        self.readers = {}
        self.known = {e: {} for e in self.eng_names}
        self.sem_owner = {}
        self.stopped = False

    def _new_sem(self, name, owner):
        s = self.stack.enter_context(self.nc.semaphore(name))
        self.sem_owner[id(s)] = owner
        return s

    def _prog_token(self, e):
        k = self.count[e]
        ep, off = divmod(k, EPOCH)
        while len(self.prog_sems[e]) <= ep:
            self.prog_sems[e].append(self._new_sem(f"p_{e}_{len(self.prog_sems[e])}", e))
        self.count[e] = k + 1
        return (self.prog_sems[e][ep], off + 1)

    def _deps(self, e, reads, writes):
        best = {}

        def add(t):
            s, v = t
            if e == "pe" and self.sem_owner[id(s)] == "pe":
                return
            cur = best.get(id(s))
            if cur is None or v > cur[1]:
                best[id(s)] = (s, v)

        for r in reads:
            t = self.last_w.get(r)
            if t is not None:
                add(t)
            if len(r) == 2 and r[0] == "B":
                for t in self.readers.get(r, {}).values():
                    if self.sem_owner[id(t[0])] != e:
                        add(t)
        for w in writes:
            t = self.last_w.get(w)
            if t is not None:
                add(t)
            for t in self.readers.get(w, {}).values():
                add(t)
        waits = []
        kn = self.known[e]
        for sid, (s, v) in best.items():
            if kn.get(sid, 0) >= v:
                continue
            kn[sid] = v
            waits.append((s, v))
        return waits

    def _commit(self, tok, reads, writes):
        for r in reads:
            d = self.readers.setdefault(r, {})
            cur = d.get(id(tok[0]))
            if cur is None or tok[1] > cur[1]:
                d[id(tok[0])] = tok
        for w in writes:
            self.last_w[w] = tok
            self.readers[w] = {}

    def op(self, e, meth, reads=(), writes=(), **kw):
        if self.stopped:
            return
        waits = self._deps(e, reads, writes)
        tok = self._prog_token(e)
        self.ops[e].append((meth, kw, waits, tok, 1))
        self._commit(tok, reads, writes)

    def dma(self, e, semkey, out, in_, reads=(), writes=()):
        if self.stopped:
            return
        waits = self._deps(e, reads, writes)
        ent = self.dma_sems.get(semkey)
        if ent is None:
            ent = [self._new_sem(f"d_{len(self.dma_sems)}", "dma"), 0]
            self.dma_sems[semkey] = ent
        ent[1] += 16
        tok = (ent[0], ent[1])
        self.ops[e].append(("dma_start", dict(out=out, in_=in_), waits, tok, 16))
        self._commit(tok, reads, writes)

    def dma_multi(self, e, semkey, pairs, reads=(), writes=()):
        if self.stopped:
            return
        waits = self._deps(e, reads, writes)
        ent = self.dma_sems.get(semkey)
        if ent is None:
            ent = [self._new_sem(f"d_{len(self.dma_sems)}", "dma"), 0]
            self.dma_sems[semkey] = ent
        tok = None
        for n, (out, in_) in enumerate(pairs):
            ent[1] += 16
            tok = (ent[0], ent[1])
            self.ops[e].append(("dma_start", dict(out=out, in_=in_), waits if n == 0 else [], tok, 16))
        self._commit(tok, reads, writes)

    def wait_all(self, e, keys):
        waits = self._deps(e, list(keys), [])
        self.ops[e].append((None, None, waits, None, 0))

    def emit(self):
        nc = self.nc
        handles = {"pe": "tensor", "act": "scalar", "dve": "vector", "pool": "gpsimd", "sp": "sync"}
        with nc.Block() as block:
            for e in self.eng_names:
                ops = self.ops[e]
                if not ops:
                    continue

                def body(eng, ops=ops):
                    for (meth, kw, waits, tok, inc) in ops:
                        for (s, v) in waits:
                            eng.wait_ge(s, v)
                        if meth is not None:
                            ins = getattr(eng, meth)(**kw)
                            ins.then_inc(tok[0], inc)

                getattr(block, handles[e])(body)


def build_nc(nlayers=DEPTH, nblk=64, stop=None):
    S = nblk * 128
    nc = bass.Bass("TRN2", target_bir_lowering=False)
    x_in = nc.dram_tensor("x", [S, D], F32, kind="ExternalInput").ap()
    pos_in = nc.dram_tensor("pos", [64, 128], I32, kind="ExternalInput").ap()
    w_in = nc.dram_tensor("w_in", [DEPTH, D, IN_W], F32, kind="ExternalInput").ap()
    w_out = nc.dram_tensor("w_out", [DEPTH, D, D], F32, kind="ExternalInput").ap()
    w_up = nc.dram_tensor("w_up", [DEPTH, D, DFF], F32, kind="ExternalInput").ap()
    w_down = nc.dram_tensor("w_down", [DEPTH, DFF, D], F32, kind="ExternalInput").ap()
    consts_in = nc.dram_tensor("consts", [128, NCON], F32, kind="ExternalInput").ap()
    gcols_in = nc.dram_tensor("gcols", [128, DEPTH * 3 * 8], F32, kind="ExternalInput").ap()
    sinks_in = nc.dram_tensor("sinksb", [128, DEPTH * 6], F32, kind="ExternalInput").ap()
    gbc_in = nc.dram_tensor("gbc", [DEPTH * 2, 128, D], F32, kind="ExternalInput").ap()
    grows_in = nc.dram_tensor("grows", [DEPTH, 5 * D], F32, kind="ExternalInput").ap()
    y_out = nc.dram_tensor("y", [S, D], F32, kind="ExternalOutput").ap()
    t0tab = nc.dram_tensor("t0tab", [2 * DEPTH + 1, D], F32, kind="Internal").ap()
    cstab = nc.dram_tensor("cstab", [64 * 128, 128], F32, kind="Internal").ap()

    with ExitStack() as st:
        Sc = Sched(nc, st)

        def sb(name, shape, dt):
            return st.enter_context(nc.sbuf_tensor("s_" + name, shape, dt))

        ps = st.enter_context(nc.psum_tensor("ps", [128, 4096], F32))

        def bank(b, w=512, off=0):
            return ps[:, b * 512 + off:b * 512 + off + w]

        def bankbf(b):
            return ps[:, b * 512:(b + 1) * 512].bitcast(BF16)

        BK = [f"B{b}" for b in range(8)]

        big = sb("big", [128, 65536], BF16)
        cst = sb("cst", [128, NCON], F32)
        identb = sb("identb", [128, 128], BF16)
        uinclb = sb("uinclb", [128, 128], BF16)
        mstrb = sb("mstrb", [128, 128], BF16)
        mcurb = sb("mcurb", [128, 128], BF16)
        mprevb = sb("mprevb", [128, 128], BF16)
        onesb = sb("onesb", [128, 2], BF16)
        gcols = sb("gcols", [128, DEPTH * 3 * 8], F32)
        esink = sb("esink", [128, DEPTH * 6], F32)
        gbc = sb("gbc", [128, D], F32)
        xtb = sb("xt", [128, 2, D], F32)
        ytmp = sb("ytmp", [128, D], F32)
        xgT = sb("xgT", [128, 8, 128], BF16)
        csb = sb("cs", [128, 2, 128], F32)
        stt = sb("stt", [128, 12], F32)
        mv = sb("mv", [128, 2], F32)
        sc1 = sb("sc1", [128, 8], F32)
        U = sb("U", [128, 3072], F32)
        hT = U[:, 0:2048].bitcast(BF16).rearrange("p (f t) -> p f t", f=32)
        rtmp = U[:, 2048:3072].rearrange("p (r n) -> p r n", r=2)
        Et = U[:, 0:1024]
        ropA = U[:, 1024:1792]
        ropB = U[:, 1792:2560]
        onesq = sb("onesq", [128, 128], BF16)
        qbTp = sb("qbTp", [128, 4, 128], BF16)
        qa_p = sb("qa_p", [128, 384], BF16)
        ka_r = sb("ka_r", [128, 128], BF16)
        va_e = sb("va_e", [128, 2, 2, 66], BF16)
        qb_s = sb("qb_s", [128, 256], BF16)
        kb_n = sb("kb_n", [128, 256], BF16)
        qc_r = sb("qc_r", [128, 384], BF16)
        qcd = sb("qcd", [128, 384], BF16)
        kc_r = sb("kc_r", [128, 384], BF16)
        kdk = sb("kdk", [128, 384], BF16)
        vc_s = sb("vc_s", [128, 384], BF16)
        sil = sb("sil", [128, 384], F32)
        silt = sb("silt", [128, 384], F32)
        qaT = sb("qaT", [128, 3, 128], BF16)
        kaT = sb("kaT", [128, 2, 128], BF16)
        qcT = sb("qcT", [128, 3, 128], BF16)
        kcT = sb("kcT", [128, 3, 128], BF16)
        qcdT = sb("qcdT", [64, 6, 128], BF16)
        Lt = sb("Lt", [128, 2, 1024], BF16)
        Wt = sb("Wt", [128, 2, 1024], BF16)
        Oacc = sb("Oacc", [128, 256], F32)
        gt = sb("gt", [128, 2, 4], F32)
        PT = sb("PT", [128, 2, 768], BF16)
        oa = sb("oa", [128, 384], F32)
        den = sb("den", [128, 6], F32)
        intraT = sb("intraT", [128, 768], BF16)
        state = sb("state", [64, 384], F32)
        stateb = sb("stateb", [64, 384], BF16)
        gst = sb("gst", [128, 36], F32)
        gmv = sb("gmv", [128, 12], F32)
        grs = sb("grs", [128, 6], F32)
        oc = sb("oc", [128, 384], F32)
        mixed = sb("mixed", [128, D], BF16)
        mixedT = sb("mixedT", [128, 8, 128], BF16)

        W_IN = big[:, 0:8 * IN_W].rearrange("p (c n) -> p c n", c=8)
        o1 = 8 * IN_W
        W_OUT = big[:, o1:o1 + 8192].rearrange("p (c n) -> p c n", c=8)
        o2 = o1 + 8192
        KBT = big[:, o2:o2 + 16384].rearrange("p (c n) -> p c n", c=2)
        o3 = o2 + 16384
        VC = big[:, o3:o3 + 16384].rearrange("p (b n) -> p b n", b=64)
        W_UP = big[:, 0:32768].rearrange("p (c n) -> p c n", c=8)
        W_DN = big[:, 32768:65536].rearrange("p (c n) -> p c n", c=32)
        T0KEYS = ["t0x", "t0xg", "t0p", "t0m", "t0h", "t0y", "t0g", "t0s", "t0w1", "t0w2", "t0c", "t0sa", "t0sb"]
        BIGKEYS = ["w_in", "w_out", "kbT", "vcache", "w_up", "w_dn", "startup", "w32", "x32", "qk32"] + T0KEYS
        W32 = big[:, o3:o3 + 12288].bitcast(F32).rearrange("p (c n) -> p c n", c=8)
        X32 = big[:, o2 + 1024:o2 + 3072].bitcast(F32).rearrange("p (c t) -> p c t", c=8)
        QK32 = big[0:64, o2 + 3072:o2 + 6144].bitcast(F32).rearrange("p (h t) -> p h t", h=12)

        def bc_last(ap, n):
            sh = list(ap.shape)
            return ap.unsqueeze(len(sh)).to_broadcast(sh + [n])

        def bc_mid(ap, n):
            sh = list(ap.shape)
            return ap.unsqueeze(1).to_broadcast([sh[0], n] + sh[1:])

        Sc.dma("sp", "cst", cst[:], consts_in[:, :], writes=["cst"])
        Sc.dma("sp", "gcols", gcols[:], gcols_in[:, :], writes=["gcols"])
        Sc.dma("sp", "esink", esink[:], sinks_in[:, :], writes=["esink"])
        for (dst, col, nm) in [(identb, C_ID, "identb"), (uinclb, C_UI, "uinclb"), (mstrb, C_MS, "mstrb"),
                               (mcurb, C_MC, "mcurb"), (mprevb, C_MP, "mprevb")]:
            Sc.op("pool", "tensor_copy", reads=["cst"], writes=[nm], out=dst[:], in_=cst[:, col:col + 128])
        Sc.op("pool", "tensor_copy", reads=["cst"], writes=["onesb"], out=onesb[:], in_=cst[:, C_ONE:C_ONE + 2])
        Sc.op("pool", "memset", writes=["va_e0", "va_e1"], ap=va_e[:], constant=1.0)
        Sc.op("pool", "memset", writes=["onesq"], ap=onesq[:], constant=1.0)
        Sc.op("pool", "memset", writes=["qbTp"], ap=qbTp[:], constant=0.0)
        Sc.op("act", "activation", reads=["esink"], writes=["esink"], out=esink[:], in_=esink[:], func=AF.Exp)

        bigf = big[:].bitcast(F32)
        posi = sb("posi", [64, 128], I32)
        posf = sb("posf", [64, 128], F32)
        posT = sb("posT", [128, 64], F32)
        ang = bigf[:, 0:4096].rearrange("p (n j i) -> p n j i", n=64, j=2)
        tq = bigf[:, 4096:8192].rearrange("p (n j i) -> p n j i", n=64, j=2)
        tki = bigf[:, 8192:12288].bitcast(I32).rearrange("p (n j i) -> p n j i", n=64, j=2)
        tkf = bigf[:, 12288:16384].rearrange("p (n j i) -> p n j i", n=64, j=2)
        tab = bigf[:, 16384:24576].rearrange("p (n c) -> p n c", n=64)
        Sc.dma("sp", "posi", posi[:], pos_in[:, :], writes=["posi"])
        Sc.op("dve", "tensor_copy", reads=["posi"], writes=["posf"], out=posf[:], in_=posi[:])
        Sc.op("pe", "transpose", reads=["posf", "cst"], writes=[BK[0]], out=bank(0, 64), in_=posf[:],
              identity=cst[0:64, C_ID:C_ID + 64])
        Sc.op("dve", "tensor_copy", reads=[BK[0]], writes=["posT"], out=posT[:], in_=bank(0, 64))
        invf = cst[:, C_IF:C_IF + 32]
        Sc.op("dve", "tensor_tensor", reads=["posT", "cst"], writes=["startup"], out=ang[:, :, 0, :],
              in0=bc_last(posT[:], 32), in1=bc_mid(invf, 64), op=ALU.mult)
        Sc.op("dve", "tensor_scalar", reads=["startup"], writes=["startup"], out=ang[:, :, 1, :],
              in0=ang[:, :, 0, :], scalar1=math.pi / 2, scalar2=None, op0=ALU.add)
        TWO_PI = 2 * math.pi
        C1 = 6.28125
        C2 = float(np.float32(TWO_PI - C1))
        C3 = TWO_PI - C1 - C2
        Sc.op("dve", "tensor_scalar", reads=["startup"], writes=["startup"], out=tq, in0=ang,
              scalar1=1.0 / TWO_PI, scalar2=0.5, op0=ALU.mult, op1=ALU.add)
        Sc.op("dve", "tensor_copy", reads=["startup"], writes=["startup"], out=tki, in_=tq)
        Sc.op("dve", "tensor_copy", reads=["startup"], writes=["startup"], out=tkf, in_=tki)
        for cc in (C1, C2, C3):
            Sc.op("dve", "scalar_tensor_tensor", reads=["startup"], writes=["startup"], out=ang, in0=tkf,
                  scalar=-cc, in1=ang, op0=ALU.mult, op1=ALU.add)
        Sc.op("dve", "tensor_single_scalar", reads=["startup"], writes=["startup"], out=tq, in_=ang,
              scalar=math.pi, op=ALU.is_gt)
        Sc.op("dve", "scalar_tensor_tensor", reads=["startup"], writes=["startup"], out=ang, in0=tq,
              scalar=-TWO_PI, in1=ang, op0=ALU.mult, op1=ALU.add)
        Sc.op("dve", "tensor_single_scalar", reads=["startup"], writes=["startup"], out=tq, in_=ang,
              scalar=-math.pi, op=ALU.is_lt)
        Sc.op("dve", "scalar_tensor_tensor", reads=["startup"], writes=["startup"], out=ang, in0=tq,
              scalar=TWO_PI, in1=ang, op0=ALU.mult, op1=ALU.add)
        PI_S = 3.1415925
        Sc.op("dve", "tensor_scalar", reads=["startup"], writes=["startup"], out=ang, in0=ang,
              scalar1=PI_S, scalar2=-PI_S, op0=ALU.min, op1=ALU.max)
        Sc.op("act", "activation", reads=["startup"], writes=["startup"], out=tkf, in_=ang, func=AF.Sin)
        Sc.op("dve", "tensor_copy", reads=["startup"], writes=["startup"], out=tab[:, :, 0:32], in_=tkf[:, :, 1, :])
        Sc.op("dve", "tensor_copy", reads=["startup"], writes=["startup"], out=tab[:, :, 32:64], in_=tkf[:, :, 1, :])
        Sc.op("dve", "tensor_scalar", reads=["startup"], writes=["startup"], out=tab[:, :, 64:96],
              in0=tkf[:, :, 0, :], scalar1=-1.0, scalar2=None, op0=ALU.mult)
        Sc.op("dve", "tensor_copy", reads=["startup"], writes=["startup"], out=tab[:, :, 96:128], in_=tkf[:, :, 0, :])
        cst_v = cstab.rearrange("(n p) c -> p n c", p=128)
        Sc.dma_multi("sp", "cstab", [(cst_v[:, 4 * q:4 * q + 4, :], tab[:, 4 * q:4 * q + 4, :]) for q in range(16)],
                     reads=["startup"], writes=["cstab"])


        def t0_path():
            AX = mybir.AxisListType.X
            bf = big[:].bitcast(F32)
            STG = [bf[:, 0:4096], bf[:, 4096:8192]]
            SK = ["t0sa", "t0sb"]
            off = [8192]

            def row(n):
                a = bf[0:1, off[0]:off[0] + n]
                off[0] += n
                return a

            t_x, t_xg, t_proj, t_mix, t_h, t_y = row(1024), row(1024), row(2944), row(1024), row(4096), row(1024)
            t_g, t_s, t_w1, t_w2 = row(5 * 1024), row(64), row(1024), row(1024)
            cb = off[0]
            t_xT = bf[:, cb:cb + 8]
            t_hT = bf[:, cb + 8:cb + 40]
            one2 = cst[0:1, C_ONE:C_ONE + 2]

            def sc(a, b):
                return t_s[:, a:b]

            def rstd_of(src, n, slot, keys):
                Sc.op("dve", "tensor_tensor", reads=keys, writes=["t0w2"], out=t_w2[:, 0:n], in0=src, in1=src, op=ALU.mult)
                Sc.op("dve", "tensor_reduce", reads=["t0w2"], writes=["t0s"], out=sc(slot, slot + 1), in_=t_w2[:, 0:n],
                      axis=AX, op=ALU.add)
                Sc.op("act", "activation", reads=["t0s"], writes=["t0s"], out=sc(slot, slot + 1), in_=sc(slot, slot + 1),
                      func=AF.Ln, bias=EPS, scale=1.0 / n)
                Sc.op("act", "activation", reads=["t0s"], writes=["t0s"], out=sc(slot, slot + 1), in_=sc(slot, slot + 1),
                      func=AF.Exp, scale=-0.5)

            def to_cols(src, nch, dst, keys):
                for c in range(nch):
                    Sc.op("pe", "matmul", reads=keys + ["cst"], writes=[BK[6]], out=ps[:, 6 * 512 + 2 * c:6 * 512 + 2 * c + 2],
                          lhsT=src[:, c * 128:(c + 1) * 128], rhs=one2, start=True, stop=True)
                Sc.op("dve", "tensor_copy", reads=[BK[6]], writes=["t0c"], out=dst,
                      in_=ps[:, 6 * 512:6 * 512 + 2 * nch].rearrange("p (c k) -> p c k", k=2)[:, :, 0])

            gi_ctr = [0]

            def stage_load(dram_ap, kc, w):
                k = gi_ctr[0] % 2
                gi_ctr[0] += 1
                v = STG[k][:, 0:kc * w].rearrange("p (c n) -> p c n", c=kc)
                Sc.dma("sp", "t0stg%d" % k, v, dram_ap, writes=[SK[k]])
                return v, SK[k]

            Sc.dma("sp", "t0ldx", t_x, x_in[0:1, :], writes=BIGKEYS)
            for l in range(nlayers):
                Sc.dma("sp", "t0ld", t_g, grows_in[l:l + 1, :], writes=["t0g"])
                g_mixpre, g_branch, g_mixpost, g_mlppre, g_mlppost = [t_g[:, k * 1024:(k + 1) * 1024] for k in range(5)]
                rstd_of(t_x, 1024, 0, ["t0x"])
                Sc.op("dve", "tensor_tensor", reads=["t0x", "t0g"], writes=["t0xg"], out=t_xg, in0=t_x, in1=g_mixpre, op=ALU.mult)
                to_cols(t_xg, 8, t_xT, ["t0xg"])
                for gi in range(6):
                    w = 512 if gi < 5 else 384
                    v, vk = stage_load(w_in[l, :, gi * 512:gi * 512 + w].rearrange("(c p) n -> p c n", p=128), 8, w)
                    for c in range(8):
                        Sc.op("pe", "matmul", reads=["t0c", vk], writes=[BK[gi % 6]], out=ps[0:1, (gi % 6) * 512:(gi % 6) * 512 + w],
                              lhsT=t_xT[:, c:c + 1], rhs=v[:, c, :], start=(c == 0), stop=(c == 7))
                    Sc.op("dve", "tensor_scalar", reads=[BK[gi % 6], "t0s"], writes=["t0p"], out=t_proj[:, gi * 512:gi * 512 + w],
                          in0=ps[0:1, (gi % 6) * 512:(gi % 6) * 512 + w], scalar1=sc(0, 1), scalar2=None, op0=ALU.mult)
                qa4 = t_proj[:, 0:384].rearrange("p (g j d) -> p g j d", g=2, j=3)
                ka4 = t_proj[:, 384:512].rearrange("p (g d) -> p g d", g=2).unsqueeze(2).to_broadcast([1, 2, 3, 64])
                va4 = t_proj[:, 512:640].rearrange("p (g d) -> p g d", g=2).unsqueeze(2).to_broadcast([1, 2, 3, 64])
                w14 = t_w1[:, 0:384].rearrange("p (g j d) -> p g j d", g=2, j=3)
                w16 = t_w1[:, 0:384].rearrange("p (h d) -> p h d", h=6)
                w26 = t_w2[:, 0:384].rearrange("p (h d) -> p h d", h=6)
                Sc.op("dve", "tensor_tensor", reads=["t0p"], writes=["t0w1"], out=w14, in0=qa4, in1=ka4, op=ALU.mult)
                Sc.op("dve", "tensor_reduce", reads=["t0w1"], writes=["t0s"], out=sc(8, 14), in_=w16, axis=AX, op=ALU.add)
                Sc.op("act", "activation", reads=["t0s"], writes=["t0s"], out=sc(16, 22), in_=sc(8, 14), func=AF.Exp, scale=0.125)
                Sc.op("dve", "tensor_tensor", reads=["t0s", "esink"], writes=["t0s"], out=sc(48, 54), in0=sc(16, 22),
                      in1=esink[0:1, l * 6:(l + 1) * 6], op=ALU.add)
                Sc.op("dve", "reciprocal", reads=["t0s"], writes=["t0s"], out=sc(48, 54), in_=sc(48, 54))
                Sc.op("dve", "tensor_tensor", reads=["t0s"], writes=["t0s"], out=sc(16, 22), in0=sc(16, 22), in1=sc(48, 54), op=ALU.mult)
                p4 = sc(16, 22).rearrange("p (g j) -> p g j", g=2).unsqueeze(3).to_broadcast([1, 2, 3, 64])
                Sc.op("dve", "tensor_tensor", reads=["t0p", "t0s"], writes=["t0w1"], out=w14, in0=va4, in1=p4, op=ALU.mult)
                rstd_of(t_w1[:, 0:384], 384, 1, ["t0w1"])
                Sc.op("dve", "scalar_tensor_tensor", reads=["t0w1", "t0s", "t0g"], writes=["t0m"], out=t_mix[:, 0:384],
                      in0=t_w1[:, 0:384], scalar=sc(1, 2), in1=g_branch[:, 0:384], op0=ALU.mult, op1=ALU.mult)
                Sc.op("dve", "memset", writes=["t0m"], ap=t_mix[:, 384:640], constant=0.0)
                qc6 = t_proj[:, 1408:1792].rearrange("p (h d) -> p h d", h=6)
                kc6 = t_proj[:, 1792:2176].rearrange("p (h d) -> p h d", h=6)
                vc6 = t_proj[:, 2176:2560].rearrange("p (h d) -> p h d", h=6)
                gcr = t_proj[:, 2560:2944]
                Sc.op("dve", "tensor_tensor", reads=["t0p"], writes=["t0w1"], out=w16, in0=qc6, in1=kc6, op=ALU.mult)
                Sc.op("dve", "tensor_reduce", reads=["t0w1"], writes=["t0s"], out=sc(24, 30), in_=w16, axis=AX, op=ALU.add)
                Sc.op("dve", "tensor_scalar", reads=["t0s"], writes=["t0s"], out=sc(24, 30), in0=sc(24, 30), scalar1=0.125,
                      scalar2=None, op0=ALU.mult)
                Sc.op("dve", "tensor_tensor", reads=["t0p", "t0s"], writes=["t0w1"], out=w16, in0=vc6, in1=bc_last(sc(24, 30), 64),
                      op=ALU.mult)
                Sc.op("dve", "tensor_reduce", reads=["t0w1"], writes=["t0s"], out=sc(32, 38), in_=w16, axis=AX, op=ALU.add)
                Sc.op("dve", "tensor_scalar", reads=["t0s"], writes=["t0s"], out=sc(32, 38), in0=sc(32, 38), scalar1=1.0 / 64,
                      scalar2=None, op0=ALU.mult)
                Sc.op("dve", "tensor_tensor", reads=["t0w1", "t0s"], writes=["t0w1"], out=w16, in0=w16, in1=bc_last(sc(32, 38), 64),
                      op=ALU.subtract)
                Sc.op("dve", "tensor_tensor", reads=["t0w1"], writes=["t0w2"], out=w26, in0=w16, in1=w16, op=ALU.mult)
                Sc.op("dve", "tensor_reduce", reads=["t0w2"], writes=["t0s"], out=sc(40, 46), in_=w26, axis=AX, op=ALU.add)
                Sc.op("act", "activation", reads=["t0s"], writes=["t0s"], out=sc(40, 46), in_=sc(40, 46), func=AF.Ln, bias=EPS,
                      scale=1.0 / 64)
                Sc.op("act", "activation", reads=["t0s"], writes=["t0s"], out=sc(40, 46), in_=sc(40, 46), func=AF.Exp, scale=-0.5)
                Sc.op("dve", "tensor_tensor", reads=["t0w1", "t0s"], writes=["t0w1"], out=w16, in0=w16, in1=bc_last(sc(40, 46), 64),
                      op=ALU.mult)
                Sc.op("dve", "tensor_tensor", reads=["t0w1", "t0g"], writes=["t0w1"], out=t_w1[:, 0:384], in0=t_w1[:, 0:384],
                      in1=g_branch[:, 640:1024], op=ALU.mult)
                Sc.op("act", "activation", reads=["t0p"], writes=["t0w2"], out=t_w2[:, 0:384], in_=gcr, func=AF.Exp, scale=-1.0)
                Sc.op("dve", "tensor_scalar", reads=["t0w2"], writes=["t0w2"], out=t_w2[:, 0:384], in0=t_w2[:, 0:384], scalar1=1.0,
                      scalar2=None, op0=ALU.add)
                Sc.op("dve", "reciprocal", reads=["t0w2"], writes=["t0w2"], out=t_w2[:, 0:384], in_=t_w2[:, 0:384])
                Sc.op("dve", "tensor_tensor", reads=["t0w2", "t0p"], writes=["t0w2"], out=t_w2[:, 0:384], in0=t_w2[:, 0:384],
                      in1=gcr, op=ALU.mult)
                Sc.op("dve", "tensor_tensor", reads=["t0w1", "t0w2"], writes=["t0m"], out=t_mix[:, 640:1024], in0=t_w1[:, 0:384],
                      in1=t_w2[:, 0:384], op=ALU.mult)
                to_cols(t_mix, 8, t_xT, ["t0m"])
                for g in range(2):
                    v, vk = stage_load(w_out[l, :, g * 512:(g + 1) * 512].rearrange("(c p) n -> p c n", p=128), 8, 512)
                    for c in range(8):
                        Sc.op("pe", "matmul", reads=["t0c", vk], writes=[BK[g]], out=ps[0:1, g * 512:(g + 1) * 512],
                              lhsT=t_xT[:, c:c + 1], rhs=v[:, c, :], start=(c == 0), stop=(c == 7))
                    Sc.op("dve", "tensor_copy", reads=[BK[g]], writes=["t0y"], out=t_y[:, g * 512:(g + 1) * 512],
                          in_=ps[0:1, g * 512:(g + 1) * 512])
                rstd_of(t_y, 1024, 2, ["t0y"])
                Sc.op("dve", "scalar_tensor_tensor", reads=["t0y", "t0s", "t0g"], writes=["t0y"], out=t_y, in0=t_y, scalar=sc(2, 3),
                      in1=g_mixpost, op0=ALU.mult, op1=ALU.mult)
                Sc.op("dve", "tensor_tensor", reads=["t0y", "t0x"], writes=["t0x"], out=t_x, in0=t_x, in1=t_y, op=ALU.add)
                Sc.dma("sp", "t0st", t0tab[2 * l + 1:2 * l + 2, :], t_x, reads=["t0x"], writes=["t0tab%d" % (2 * l + 1)])
                rstd_of(t_x, 1024, 0, ["t0x"])
                Sc.op("dve", "tensor_tensor", reads=["t0x", "t0g"], writes=["t0xg"], out=t_xg, in0=t_x, in1=g_mlppre, op=ALU.mult)
                to_cols(t_xg, 8, t_xT, ["t0xg"])
                for g in range(8):
                    v, vk = stage_load(w_up[l, :, g * 512:(g + 1) * 512].rearrange("(c p) n -> p c n", p=128), 8, 512)
                    b = g % 6
                    for c in range(8):
                        Sc.op("pe", "matmul", reads=["t0c", vk], writes=[BK[b]], out=ps[0:1, b * 512:(b + 1) * 512],
                              lhsT=t_xT[:, c:c + 1], rhs=v[:, c, :], start=(c == 0), stop=(c == 7))
                    Sc.op("dve", "tensor_scalar", reads=[BK[b], "t0s"], writes=["t0w1"], out=t_w1[:, 0:512],
                          in0=ps[0:1, b * 512:(b + 1) * 512], scalar1=sc(0, 1), scalar2=0.0, op0=ALU.mult, op1=ALU.max)
                    Sc.op("dve", "tensor_tensor", reads=["t0w1"], writes=["t0h"], out=t_h[:, g * 512:(g + 1) * 512],
                          in0=t_w1[:, 0:512], in1=t_w1[:, 0:512], op=ALU.mult)
                to_cols(t_h, 32, t_hT, ["t0h"])
                for g in range(8):
                    v, vk = stage_load(w_down[l, g * 512:(g + 1) * 512, :].rearrange("(c p) n -> p c n", p=128), 4, 1024)
                    for k in range(4):
                        fc = g * 4 + k
                        for half in range(2):
                            Sc.op("pe", "matmul", reads=["t0c", vk], writes=[BK[half]], out=ps[0:1, half * 512:(half + 1) * 512],
                                  lhsT=t_hT[:, fc:fc + 1], rhs=v[:, k, half * 512:(half + 1) * 512],
                                  start=(fc == 0), stop=(fc == 31))
                for half in range(2):
                    Sc.op("dve", "tensor_copy", reads=[BK[half]], writes=["t0y"], out=t_y[:, half * 512:(half + 1) * 512],
                          in_=ps[0:1, half * 512:(half + 1) * 512])
                rstd_of(t_y, 1024, 2, ["t0y"])
                Sc.op("dve", "scalar_tensor_tensor", reads=["t0y", "t0s", "t0g"], writes=["t0y"], out=t_y, in0=t_y, scalar=sc(2, 3),
                      in1=g_mlppost, op0=ALU.mult, op1=ALU.mult)
                Sc.op("dve", "tensor_tensor", reads=["t0y", "t0x"], writes=["t0x"], out=t_x, in0=t_x, in1=t_y, op=ALU.add)
                Sc.dma("sp", "t0st", t0tab[2 * l + 2:2 * l + 3, :], t_x, reads=["t0x"], writes=["t0tab%d" % (2 * l + 2)])

        t0_path()
        def rstd_from_mv(mv_ap, dst, key_in, key_out, scale=1.0):
            Sc.op("dve", "scalar_tensor_tensor", reads=[key_in], writes=[key_out], out=dst, in0=mv_ap[:, 0:1],
                  scalar=mv_ap[:, 0:1], in1=mv_ap[:, 1:2], op0=ALU.mult, op1=ALU.add)
            Sc.op("act", "activation", reads=[key_out], writes=[key_out], out=dst, in_=dst, func=AF.Ln,
                  bias=EPS, scale=scale)
            Sc.op("act", "activation", reads=[key_out], writes=[key_out], out=dst, in_=dst, func=AF.Exp,
                  scale=-0.5)

        sqj = sb("sqj", [128, 512], BF16)
        ssq = sb("ssq", [128, 16], F32)

        def rstd_sq(srcs, n_tot, dst, keys_in, key_out, slot, scale_ap=None, scale_key=None):
            for k, ap in enumerate(srcs):
                Sc.op("act", "activation", reads=keys_in, writes=["sqj", "ssq%d_%d" % (slot, k)], out=sqj[:, 0:ap.shape[1]], in_=ap,
                      func=AF.Square, accum_out=ssq[:, slot + k:slot + k + 1])
            src = ssq[:, slot:slot + 1]
            rk = ["ssq%d_%d" % (slot, k) for k in range(len(srcs))]
            if len(srcs) == 2:
                Sc.op("pool", "tensor_tensor", reads=rk, writes=["ssq%d_0" % slot], out=src, in0=src,
                      in1=ssq[:, slot + 1:slot + 2], op=ALU.add)
                rk = ["ssq%d_0" % slot]
            if scale_ap is None:
                Sc.op("act", "activation", reads=rk, writes=[key_out], out=dst, in_=src, func=AF.Ln, bias=EPS,
                      scale=1.0 / n_tot)
            else:
                Sc.op("act", "activation", reads=rk + [scale_key], writes=[key_out], out=dst, in_=src, func=AF.Ln, bias=EPS,
                      scale=scale_ap)
            Sc.op("act", "activation", reads=[key_out], writes=[key_out], out=dst, in_=dst, func=AF.Exp, scale=-0.5)

        def row_stats(src_halves, key_src, n):
            for k, ap in enumerate(src_halves):
                Sc.op("dve", "bn_stats", reads=key_src, writes=["stt"], out=stt[:, 6 * k:6 * k + 6], in_=ap)
            Sc.op("dve", "bn_aggr", reads=["stt"], writes=["mv"], out=mv[:], in_=stt[:, 0:6 * n])

        def load_x(src_ap, i, with_cs, t0row=None):
            bf_ = i % 2
            xk = "xt%d" % bf_
            Sc.dma("sp", xk, xtb[:, bf_, :], src_ap[i * 128:(i + 1) * 128, :], reads=["xsrc%d" % i], writes=[xk])
            if t0row is not None:
                Sc.dma("sp", xk, xtb[0:1, bf_, :], t0tab[t0row:t0row + 1, :], reads=["t0tab%d" % t0row], writes=[xk])
            if with_cs:
                Sc.dma("sp", "cs%d" % bf_, csb[:, bf_, :], cstab[i * 128:(i + 1) * 128, :], reads=["cstab"],
                       writes=["cs%d" % bf_])

        def stats_transpose(i, gcol_base, x32=False, want_s1=False):
            bf_ = i % 2
            xk = "xt%d" % bf_
            xt = xtb[:, bf_, :]
            rstd_sq([xt[:, 0:512], xt[:, 512:1024]], 1024, sc1[:, 0:1], [xk], "rstd", 0)
            if want_s1:
                Sc.op("dve", "tensor_tensor", reads=["rstd"], writes=["s1_%d" % bf_], out=sc1[:, 6 + bf_:7 + bf_],
                      in0=sc1[:, 0:1], in1=sc1[:, 0:1], op=ALU.mult)
                Sc.op("dve", "scalar_tensor_tensor", reads=["s1_%d" % bf_], writes=["s1q_%d" % bf_], out=ssq[:, 12 + bf_:13 + bf_],
                      in0=sc1[:, 6 + bf_:7 + bf_], scalar=1.0 / 1024, in1=sc1[:, 6 + bf_:7 + bf_], op0=ALU.mult, op1=ALU.mult)
            for half in range(2):
                b = half
                for c4 in range(4):
                    c = half * 4 + c4
                    Sc.op("pe", "transpose", reads=[xk, "cst"], writes=[BK[b]], out=bank(b, 128, c4 * 128),
                          in_=xt[:, c * 128:(c + 1) * 128], identity=cst[:, C_ID:C_ID + 128])
                Sc.op("dve", "tensor_tensor", reads=[BK[b], "gcols"], writes=["xgT"], out=xgT[:, half * 4:(half + 1) * 4, :],
                      in0=bank(b).rearrange("p (c t) -> p c t", c=4),
                      in1=bc_last(gcols[:, gcol_base + half * 4:gcol_base + half * 4 + 4], 128), op=ALU.mult)
                for c4 in range(4):
                    c = half * 4 + c4
                    if x32:
                        Sc.op("dve", "tensor_scalar", reads=[BK[b], "gcols"], writes=["x32"], out=X32[:, c, :],
                              in0=bank(b, 128, c4 * 128), scalar1=gcols[:, gcol_base + c:gcol_base + c + 1],
                              scalar2=None, op0=ALU.mult)

        def residual_out(ybanks, scale_ap, scale_key, i):
            bf_ = i % 2
            xk = "xt%d" % bf_
            xt = xtb[:, bf_, :]
            for half, b in enumerate(ybanks):
                Sc.op("dve", "scalar_tensor_tensor", reads=[BK[b], scale_key, "gbc"], writes=["ytmp"],
                      out=ytmp[:, half * 512:(half + 1) * 512], in0=bank(b), scalar=scale_ap,
                      in1=gbc[:, half * 512:(half + 1) * 512], op0=ALU.mult, op1=ALU.mult)
            Sc.op("pool", "tensor_tensor", reads=["ytmp", xk], writes=[xk], out=xt, in0=xt, in1=ytmp[:],
                  op=ALU.add)
            Sc.dma("sp", "xst%d" % bf_, y_out[i * 128:(i + 1) * 128, :], xt, reads=[xk], writes=["xsrc%d" % i])

        def transpose_bf(src_ap, ncols, b, slot, rd):
            Sc.op("pe", "transpose", reads=rd + ["identb"], writes=[BK[b]],
                  out=bankbf(b)[0:ncols, slot * 128:(slot + 1) * 128], in_=src_ap, identity=identb[:])

        def stage(k):
            if stop is not None and k >= stop:
                Sc.stopped = True

        for l in range(nlayers):
            x_src = x_in if l == 0 else y_out
            prs = []
            for c in range(8):
                for hh in range(2):
                    prs.append((W_IN[:, c, hh * 1472:(hh + 1) * 1472],
                                w_in[l, c * 128:(c + 1) * 128, hh * 1472:(hh + 1) * 1472]))
            for c in range(8):
                prs.append((W_OUT[:, c, :], w_out[l, c * 128:(c + 1) * 128, :]))
            Sc.dma_multi("pool", "wld", prs, writes=BIGKEYS)
            Sc.dma("sp", "w32", W32, w_in[l, :, 1408:2176].rearrange("(c p) n -> p c n", p=128), writes=BIGKEYS)
            Sc.dma("sp", "gbc", gbc[:], gbc_in[l * 2 + 0], writes=["gbc"])
            Sc.op("pool", "memset", writes=["state"], ap=state[:], constant=0.0)
            gb = l * 24
            es = esink[:, l * 6:(l + 1) * 6]

            load_x(x_src, 0, True, t0row=(2 * l if l > 0 else None))
            for i in range(nblk):
                r = i % 2
                cs = csb[:, r, :]
                csk = "cs%d" % r
                if i + 1 < nblk:
                    load_x(x_src, i + 1, True)
                if i == 0:
                    stats_transpose(i, gb + 0, x32=True)
                stage(1)
                for b6 in range(6):
                    w = 512 if b6 < 5 else 384
                    for c in range(8):
                        Sc.op("pe", "matmul", reads=["xgT", "w_in"], writes=[BK[2 + b6]], out=bank(2 + b6, w),
                              lhsT=xgT[:, c, :], rhs=W_IN[:, c, b6 * 512:b6 * 512 + w], start=(c == 0), stop=(c == 7))
                if i == 0:
                    for (b32, c0, w) in ((0, 0, 512), (1, 512, 256)):
                        for c in range(8):
                            Sc.op("pe", "matmul", reads=["x32", "w32"], writes=[BK[b32]], out=bank(b32, w),
                                  lhsT=X32[:, c, :], rhs=W32[:, c, c0:c0 + w], start=(c == 0), stop=(c == 7))
                stage(2)
                PJ = ps[:, 1024:1024 + 3072]
                pjk = BK[2:8]

                def pk(c0, c1):
                    return [BK[2 + b] for b in range(c0 // 512, (c1 - 1) // 512 + 1)]
                rstd = sc1[:, 0:1]
                Sc.op("dve", "tensor_scalar", reads=["rstd"], writes=["rstdv"], out=sc1[:, 1:2], in0=rstd,
                      scalar1=-0.125, scalar2=None, op0=ALU.mult)
                Sc.op("dve", "tensor_scalar", reads=["rstd"], writes=["rstdv"], out=sc1[:, 2:3], in0=rstd,
                      scalar1=-1.0, scalar2=None, op0=ALU.mult)

                def rope(col0, nh, outs, srcflat=None, srckeys=None):
                    if srcflat is None:
                        srcflat = PJ[:, col0:col0 + nh * 64]
                    src = srcflat.rearrange("p (h d) -> p h d", h=nh)
                    pjk = srckeys if srckeys is not None else pk(col0, col0 + nh * 64)
                    A3 = ropA[:, 0:nh * 64].rearrange("p (h d) -> p h d", h=nh)
                    B3 = ropB[:, 0:nh * 64].rearrange("p (h d) -> p h d", h=nh)
                    Sc.op("dve", "scalar_tensor_tensor", reads=pjk + ["rstd", csk], writes=["ropA"], out=A3, in0=src,
                          scalar=rstd, in1=bc_mid(cs[:, 0:64], nh), op0=ALU.mult, op1=ALU.mult)
                    Sc.op("dve", "scalar_tensor_tensor", reads=pjk + ["rstd", csk], writes=["ropB"], out=B3[:, :, 0:32],
                          in0=src[:, :, 32:64], scalar=rstd, in1=bc_mid(cs[:, 64:96], nh), op0=ALU.mult, op1=ALU.mult)
                    Sc.op("dve", "scalar_tensor_tensor", reads=pjk + ["rstd", csk], writes=["ropB"], out=B3[:, :, 32:64],
                          in0=src[:, :, 0:32], scalar=rstd, in1=bc_mid(cs[:, 96:128], nh), op0=ALU.mult, op1=ALU.mult)
                    Sc.op("pool", "tensor_tensor", reads=["ropA", "ropB"], writes=["ropA"], out=ropA[:, 0:nh * 64],
                          in0=ropA[:, 0:nh * 64], in1=ropB[:, 0:nh * 64], op=ALU.add)
                    outs()

                def outs_a():
                    dst = qa_p[:].rearrange("p (j g d) -> p g j d", j=3, g=2)
                    srcq = ropA[:, 0:384].rearrange("p (g j d) -> p g j d", g=2, j=3)
                    Sc.op("pool", "tensor_copy", reads=["ropA"], writes=["qa_p"], out=dst, in_=srcq)
                    Sc.op("pool", "tensor_copy", reads=["ropA"], writes=["ka_r"], out=ka_r[:], in_=ropA[:, 384:512])

                rope(0, 8, outs_a)
                stage(3)
                Sc.op("act", "activation", reads=pk(512, 640) + ["rstd"], writes=["va_e%d" % r],
                      out=va_e[:, r, :, 0:64], in_=PJ[:, 512:640].rearrange("p (g d) -> p g d", g=2),
                      func=AF.Copy, scale=rstd)
                Sc.op("act", "activation", reads=pk(640, 896) + ["rstd"], writes=["qb_s"], out=qb_s[:], in_=PJ[:, 640:896],
                      func=AF.Copy, scale=rstd)
                Sc.op("act", "activation", reads=pk(896, 1152) + ["rstdv"], writes=["kb_n"], out=kb_n[:], in_=PJ[:, 896:1152],
                      func=AF.Copy, scale=sc1[:, 1:2])
                Sc.op("act", "activation", reads=pk(1152, 1408) + ["rstd"], writes=["vcache", "w32"], out=VC[:, i, :],
                      in_=PJ[:, 1152:1408], func=AF.Copy, scale=rstd)
                Sc.op("act", "activation", reads=pk(2176, 2560) + ["rstd"], writes=["vc_s"], out=vc_s[:], in_=PJ[:, 2176:2560],
                      func=AF.Copy, scale=rstd)
                Sc.op("act", "activation", reads=pk(2560, 2944) + ["rstdv"], writes=["silt"], out=silt[:], in_=PJ[:, 2560:2944],
                      func=AF.Exp, scale=sc1[:, 2:3])
                Sc.op("act", "activation", reads=pk(2560, 2944) + ["rstd"], writes=["sil"], out=sil[:], in_=PJ[:, 2560:2944],
                      func=AF.Copy, scale=rstd)

                stage(4)

                def outs_c():
                    if i == 0:
                        for h12 in range(12):
                            b32 = (0, 1, 7)[h12 // 4]
                            Sc.op("pe", "transpose", reads=["ropA", "cst"], writes=[BK[b32]],
                                  out=ps[0:64, b32 * 512 + (h12 % 4) * 128:b32 * 512 + (h12 % 4 + 1) * 128],
                                  in_=ropA[:, h12 * 64:(h12 + 1) * 64], identity=cst[:, C_ID:C_ID + 128])
                        for q3 in range(3):
                            b32 = (0, 1, 7)[q3]
                            Sc.op("dve", "tensor_copy", reads=[BK[b32]], writes=["qk32"],
                                  out=QK32[:, q3 * 4:(q3 + 1) * 4, :].rearrange("p h t -> p (h t)"),
                                  in_=ps[0:64, b32 * 512:(b32 + 1) * 512])
                    Sc.op("pool", "tensor_copy", reads=["ropA"], writes=["qc_r"], out=qc_r[:], in_=ropA[:, 0:384])
                    Sc.op("pool", "tensor_copy", reads=["ropA"], writes=["kc_r"], out=kc_r[:], in_=ropA[:, 384:768])
                    Sc.op("pool", "tensor_tensor", reads=["ropA", "cst"], writes=["qcd"],
                          out=qcd[:].rearrange("p (h d) -> p h d", h=6),
                          in0=ropA[:, 0:384].rearrange("p (h d) -> p h d", h=6),
                          in1=bc_last(cst[:, C_QD:C_QD + 6], 64), op=ALU.mult)
                    Sc.op("pool", "tensor_tensor", reads=["ropA", "cst"], writes=["kdk"],
                          out=kdk[:].rearrange("p (h d) -> p h d", h=6),
                          in0=ropA[:, 384:768].rearrange("p (h d) -> p h d", h=6),
                          in1=bc_last(cst[:, C_KD:C_KD + 6], 64), op=ALU.mult)

                if i == 0:
                    rope(1408, 12, outs_c, srcflat=ps[:, 0:768], srckeys=[BK[0], BK[1]])
                else:
                    rope(1408, 12, outs_c)
                Sc.op("dve", "tensor_scalar", reads=["silt"], writes=["silt"], out=silt[:], in0=silt[:], scalar1=1.0,
                      scalar2=None, op0=ALU.add)
                Sc.op("dve", "reciprocal", reads=["silt"], writes=["silt"], out=silt[:], in_=silt[:])
                Sc.op("pool", "tensor_tensor", reads=["sil", "silt"], writes=["sil"], out=sil[:], in0=sil[:], in1=silt[:],
                      op=ALU.mult)
                stage(5)

                for j in range(3):
                    transpose_bf(qa_p[:, j * 128:(j + 1) * 128], 128, 0, j, ["qa_p"])
                transpose_bf(ka_r[:], 128, 0, 3, ["ka_r"])
                for j in range(2):
                    transpose_bf(qb_s[:, j * 128:(j + 1) * 128], 128, 0, 4 + j, ["qb_s"])
                for j in range(2):
                    transpose_bf(kb_n[:, j * 128:(j + 1) * 128], 128, 0, 6 + j, ["kb_n"])
                b0 = bankbf(0)
                Sc.op("dve", "tensor_copy", reads=[BK[0]], writes=["qaT"], out=qaT[:].rearrange("p j t -> p (j t)"),
                      in_=b0[:, 0:384])
                Sc.op("dve", "tensor_copy", reads=[BK[0]], writes=["kaT%d" % r], out=kaT[:, r, :], in_=b0[:, 384:512])
                for par in range(2):
                    Sc.op("dve", "tensor_copy", reads=[BK[0]], writes=["qbTp"],
                          out=qbTp[par * 64:(par + 1) * 64, :, :].rearrange("p (j q) t -> p j q t", q=2)[:, :, par, :],
                          in_=b0[par * 64:(par + 1) * 64, 512:768].rearrange("p (j t) -> p j t", j=2))
                Sc.op("dve", "tensor_copy", reads=[BK[0]], writes=["kbT", "x32", "qk32"], out=KBT[:, :, i * 128:(i + 1) * 128],
                      in_=b0[:, 768:1024].rearrange("p (j t) -> p j t", j=2))
                for j in range(3):
                    transpose_bf(qc_r[:, j * 128:(j + 1) * 128], 128, 1, j, ["qc_r"])
                for j in range(3):
                    transpose_bf(kc_r[:, j * 128:(j + 1) * 128], 128, 1, 3 + j, ["kc_r"])
                b1 = bankbf(1)
                Sc.op("dve", "tensor_copy", reads=[BK[1]], writes=["qcT"], out=qcT[:].rearrange("p j t -> p (j t)"),
                      in_=b1[:, 0:384])
                Sc.op("dve", "tensor_copy", reads=[BK[1]], writes=["kcT"], out=kcT[:].rearrange("p j t -> p (j t)"),
                      in_=b1[:, 384:768])
                for h in range(6):
                    transpose_bf(qcd[:, h * 64:(h + 1) * 64], 64, 0, h, ["qcd"])
                Sc.op("dve", "tensor_copy", reads=[BK[0]], writes=["qcdT"], out=qcdT[:].rearrange("p j t -> p (j t)"),
                      in_=bankbf(0)[0:64, 0:768])

                stage(6)

                def swa_gen():
                    kbs = [1] if i == 0 else [0, 1]
                    for kbi in kbs:
                        rr = r if kbi == 1 else 1 - r
                        for g in range(2):
                            b = 2 + kbi * 2 + g
                            Sc.op("pe", "matmul", reads=["kaT%d" % rr, "qaT"], writes=[BK[b]], out=bank(b, 384),
                                  lhsT=kaT[g * 64:(g + 1) * 64, rr, :], rhs=qaT[g * 64:(g + 1) * 64, :, :],
                                  start=True, stop=True)
                        yield
                    for kbi in kbs:
                        b = 2 + kbi * 2
                        Sc.op("act", "activation", reads=[BK[b], BK[b + 1]], writes=["PT%d" % kbi],
                              out=PT[:, kbi, :].rearrange("p (g n) -> p g n", g=2),
                              in_=ps[:, b * 512:(b + 2) * 512].rearrange("p (g n) -> p g n", g=2)[:, :, 0:384],
                              func=AF.Exp, scale=0.125)
                        msk = mcurb if kbi == 1 else mprevb
                        Sc.op("pool", "tensor_tensor", reads=["PT%d" % kbi, "mcurb", "mprevb"], writes=["PT%d" % kbi],
                              out=PT[:, kbi, :].rearrange("p (h t) -> p h t", h=6),
                              in0=PT[:, kbi, :].rearrange("p (h t) -> p h t", h=6), in1=bc_mid(msk[:], 6), op=ALU.mult)
                        yield
                    for h in range(6):
                        g, j = divmod(h, 3)
                        for n_, kbi in enumerate(kbs):
                            rr = r if kbi == 1 else 1 - r
                            Sc.op("pe", "matmul", reads=["PT%d" % kbi, "va_e%d" % rr], writes=[BK[6]],
                                  out=bank(6, 66, h * 66), lhsT=PT[:, kbi, (g * 3 + j) * 128:(g * 3 + j + 1) * 128],
                                  rhs=va_e[:, rr, g, :], start=(n_ == 0), stop=(n_ == len(kbs) - 1))
                    yield
                    o6 = bank(6, 396).rearrange("p (h e) -> p h e", h=6)
                    Sc.op("dve", "tensor_tensor", reads=[BK[6], "esink"], writes=["den"], out=den[:], in0=o6[:, :, 64],
                          in1=es, op=ALU.add)
                    Sc.op("dve", "reciprocal", reads=["den"], writes=["den"], out=den[:], in_=den[:])
                    Sc.op("dve", "tensor_tensor", reads=[BK[6], "den"], writes=["oa"],
                          out=oa[:].rearrange("p (h d) -> p h d", h=6), in0=o6[:, :, 0:64], in1=bc_last(den[:], 64),
                          op=ALU.mult)
                    yield
                    rstd_sq([oa[:]], 384, sc1[:, 3:4], ["oa"], "rstd_a", 2)
                    yield
                    Sc.op("dve", "tensor_scalar", reads=["oa", "rstd_a"], writes=["mixed"], out=mixed[:, 0:384], in0=oa[:],
                          scalar1=sc1[:, 3:4], scalar2=None, op0=ALU.mult)

                def ret_gen():
                    for h in range(6):
                        b = h % 2
                        pb = (h % 2) * 64
                        if i == 0:
                            Sc.op("pe", "matmul", reads=["qk32"], writes=[BK[b]], out=bank(b, 128, (h // 2) * 128),
                                  lhsT=QK32[:, 6 + h, :], rhs=QK32[:, h, :], start=True, stop=True)
                        else:
                            Sc.op("pe", "matmul", reads=["kcT", "qcT"], writes=[BK[b]], out=bank(b, 128, (h // 2) * 128),
                                  lhsT=kcT[pb:pb + 64, h // 2, :], rhs=qcT[pb:pb + 64, h // 2, :], start=True, stop=True)
                    yield
                    Sc.op("dve", "tensor_tensor", reads=[BK[0], BK[1], "cst"], writes=["intraT"],
                          out=intraT[:].rearrange("p (j par t) -> p par j t", j=3, par=2),
                          in0=ps[:, 0:1024].rearrange("p (par n) -> p par n", par=2)[:, :, 0:384].rearrange(
                              "p par (j t) -> p par j t", j=3),
                          in1=cst[:, C_DD:C_DD + 768].rearrange("p (j par t) -> p par j t", j=3, par=2), op=ALU.mult)
                    yield
                    for h in range(6):
                        Sc.op("pe", "matmul", reads=["intraT", "vc_s"], writes=[BK[7]], out=bank(7, 64, h * 64),
                              lhsT=intraT[:, h * 128:(h + 1) * 128], rhs=vc_s[:, h * 64:(h + 1) * 64],
                              start=True, stop=(i == 0))
                        if i > 0:
                            Sc.op("pe", "matmul", reads=["qcdT", "stateb"], writes=[BK[7]], out=bank(7, 64, h * 64),
                                  lhsT=qcdT[:, h, :], rhs=stateb[:, h * 64:(h + 1) * 64], start=False, stop=True)
                    for h in range(6):
                        Sc.op("pe", "matmul", reads=["kdk", "vc_s"], writes=[BK[0]], out=ps[0:64, h * 64:(h + 1) * 64],
                              lhsT=kdk[:, h * 64:(h + 1) * 64], rhs=vc_s[:, h * 64:(h + 1) * 64], start=True, stop=True)
                    yield
                    o7 = bank(7, 384).rearrange("p (h d) -> p h d", h=6)
                    Sc.op("dve", "tensor_reduce", reads=[BK[7]], writes=["gmv"], out=gmv[:, 0:6], in_=o7,
                          axis=mybir.AxisListType.X, op=ALU.add)
                    Sc.op("act", "activation", reads=[BK[7]], writes=["oc"], out=oc[:], in_=bank(7, 384), func=AF.Square)
                    yield
                    Sc.op("dve", "tensor_reduce", reads=["oc"], writes=["gst"], out=gst[:, 0:6],
                          in_=oc[:].rearrange("p (h d) -> p h d", h=6), axis=mybir.AxisListType.X, op=ALU.add)
                    Sc.op("dve", "tensor_scalar", reads=["gmv"], writes=["gmv"], out=gmv[:, 0:6], in0=gmv[:, 0:6],
                          scalar1=1.0 / 64, scalar2=None, op0=ALU.mult)
                    Sc.op("dve", "tensor_tensor", reads=["gmv"], writes=["gst2"], out=gst[:, 6:12], in0=gmv[:, 0:6],
                          in1=gmv[:, 0:6], op=ALU.mult)
                    Sc.op("dve", "scalar_tensor_tensor", reads=["gst", "gst2"], writes=["gvar"], out=gmv[:, 6:12], in0=gst[:, 0:6],
                          scalar=1.0 / 64, in1=gst[:, 6:12], op0=ALU.mult, op1=ALU.subtract)
                    Sc.op("act", "activation", reads=["gvar"], writes=["grs"], out=grs[:], in_=gmv[:, 6:12], func=AF.Ln,
                          bias=EPS, scale=1.0)
                    Sc.op("act", "activation", reads=["grs"], writes=["grs"], out=grs[:], in_=grs[:], func=AF.Exp,
                          scale=-0.5)
                    yield
                    for h in range(6):
                        cd = math.exp(128.0 * LOG_GAMMA[h])
                        Sc.op("dve", "scalar_tensor_tensor", reads=[BK[0], "state"], writes=["state"],
                              out=state[:, h * 64:(h + 1) * 64], in0=state[:, h * 64:(h + 1) * 64], scalar=cd,
                              in1=ps[0:64, h * 64:(h + 1) * 64], op0=ALU.mult, op1=ALU.add)
                    Sc.op("pool", "tensor_copy", reads=["state"], writes=["stateb"], out=stateb[:], in_=state[:])
                    yield
                    oc3 = oc[:].rearrange("p (h d) -> p h d", h=6)
                    Sc.op("dve", "tensor_tensor", reads=[BK[7], "gmv", "gst"], writes=["oc"], out=oc3,
                          in0=bank(7, 384).rearrange("p (h d) -> p h d", h=6), in1=bc_last(gmv[:, 0:6], 64),
                          op=ALU.subtract)
                    Sc.op("pool", "tensor_tensor", reads=["oc", "grs"], writes=["oc"], out=oc3, in0=oc3,
                          in1=bc_last(grs[:], 64), op=ALU.mult)
                    Sc.op("pool", "tensor_tensor", reads=["oc", "sil"], writes=["mixed"], out=mixed[:, 640:1024], in0=oc[:],
                          in1=sil[:], op=ALU.mult)

                gens = [swa_gen(), ret_gen()]
                while gens:
                    for g_ in list(gens):
                        try:
                            next(g_)
                        except StopIteration:
                            gens.remove(g_)

                stage(8)
                if DBG.get('skip_sb'):
                    steps = [[i]]
                else:
                    steps = [[i]]
                    jj = i - 1
                    while jj >= 0:
                        if jj >= 1:
                            steps.append([jj, jj - 1]); jj -= 2
                        else:
                            steps.append([jj]); jj -= 1
                c_state = [False]

                def sb_cfg(n_):
                    js = steps[n_]
                    s_ = n_ % 2
                    ab = [2 * (n_ % 3), 2 * (n_ % 3) + 1]
                    return js, s_, len(js), ab, [BK[ab[u]] for u in range(len(js))], (js[0] == i), "Lt%d" % s_, "Wt%d" % s_

                def sb_z(n_):
                    js, s, nk, ab, abk, diag, Lk, Wk = sb_cfg(n_)
                    for u, j in enumerate(js):
                        for h in range(4):
                            Sc.op("pe", "matmul", reads=["kbT", "qbTp"], writes=[BK[ab[u]]], out=bank(ab[u], 128, h * 128),
                                  lhsT=KBT[:, h // 2, j * 128:(j + 1) * 128], rhs=qbTp[:, h, :],
                                  start=(h == 0), stop=False, skip_group_check=True)

                def sb_el(n_):
                    js, s, nk, ab, abk, diag, Lk, Wk = sb_cfg(n_)
                    Aall = ps[:, ab[0] * 512:ab[0] * 512 + 512 * nk]
                    Sc.op("act", "activation", reads=abk, writes=["Et"], out=Et[:, 0:512 * nk], in_=Aall, func=AF.Exp, scale=-1.0)
                    Sc.op("act", "activation", reads=["Et"], writes=[Lk], out=Lt[:, s, 0:512 * nk], in_=Et[:, 0:512 * nk],
                          func=AF.Ln, bias=1.0, scale=1.0)
                    if diag:
                        Sc.op("pool", "tensor_tensor", reads=[Lk, "mstrb"], writes=[Lk],
                              out=Lt[:, s, 0:512].rearrange("p (h t) -> p h t", h=4),
                              in0=Lt[:, s, 0:512].rearrange("p (h t) -> p h t", h=4), in1=bc_mid(mstrb[:], 4), op=ALU.mult)

                def sb_cum(n_):
                    js, s, nk, ab, abk, diag, Lk, Wk = sb_cfg(n_)
                    for u in range(nk):
                        Sc.op("pe", "matmul", reads=[Lk, "uinclb"], writes=[BK[ab[u]]], out=bank(ab[u]), lhsT=uinclb[:],
                              rhs=Lt[:, s, u * 512:(u + 1) * 512], start=False, stop=(u == 0), skip_group_check=True)
                    if nk == 2:
                        Sc.op("pe", "matmul", reads=[Lk, "onesq"], writes=[BK[ab[1]]], out=bank(ab[1]), lhsT=onesq[:],
                              rhs=Lt[:, s, 0:512], start=False, stop=True, skip_group_check=True)

                def sb_rest(n_):
                    js, s, nk, ab, abk, diag, Lk, Wk = sb_cfg(n_)
                    pbk = 6
                    Aall = ps[:, ab[0] * 512:ab[0] * 512 + 512 * nk]
                    if not diag:
                        Sc.op("act", "activation", reads=[BK[7]], writes=["gt%d" % s], out=gt[:, s, :],
                              in_=bank(7, 8).rearrange("p (h k) -> p h k", h=4)[:, :, 0], func=AF.Exp, scale=-1.0)
                    Sc.op("act", "activation", reads=abk, writes=[Wk], out=Wt[:, s, 0:512 * nk], in_=Aall, func=AF.Exp, scale=-1.0)
                    if diag:
                        Sc.op("pool", "tensor_tensor", reads=[Wk, "mstrb"], writes=[Wk],
                              out=Wt[:, s, 0:512].rearrange("p (h t) -> p h t", h=4),
                              in0=Wt[:, s, 0:512].rearrange("p (h t) -> p h t", h=4), in1=bc_mid(mstrb[:], 4), op=ALU.mult)
                    for h in range(4):
                        for u, j in enumerate(js):
                            Sc.op("pe", "matmul", reads=[Wk, "vcache"], writes=[BK[pbk]], out=bank(pbk, 64, h * 64),
                                  lhsT=Wt[:, s, u * 512 + h * 128:u * 512 + (h + 1) * 128], rhs=VC[:, j, h * 64:(h + 1) * 64],
                                  start=(u == 0), stop=(u == nk - 1))
                    if diag:
                        Sc.op("dve", "tensor_copy", reads=[BK[pbk]], writes=["Oacc"], out=Oacc[:], in_=bank(pbk, 256))
                    else:
                        for h in range(4):
                            Sc.op("dve", "scalar_tensor_tensor", reads=[BK[pbk], "gt%d" % s, "Oacc"], writes=["Oacc"],
                                  out=Oacc[:, h * 64:(h + 1) * 64], in0=bank(pbk, 64, h * 64), scalar=gt[:, s, h:h + 1],
                                  in1=Oacc[:, h * 64:(h + 1) * 64], op0=ALU.mult, op1=ALU.add)

                def sb_colsum(n_):
                    js, s, nk, ab, abk, diag, Lk, Wk = sb_cfg(n_)
                    if js[-1] > 0:
                        for u in range(nk):
                            for h in range(4):
                                Sc.op("pe", "matmul", reads=[Lk, "onesb"], writes=[BK[7]], out=bank(7, 2, h * 2),
                                      lhsT=Lt[:, s, u * 512 + h * 128:u * 512 + (h + 1) * 128], rhs=onesb[:],
                                      start=(not c_state[0]), stop=False, skip_group_check=True)
                                c_state[0] = True

                NS = len(steps)
                sb_z(0)
                if NS > 1:
                    sb_z(1)
                sb_el(0)
                for n_ in range(NS):
                    sb_cum(n_)
                    if n_ + 2 < NS:
                        sb_z(n_ + 2)
                    if n_ + 1 < NS:
                        sb_el(n_ + 1)
                    sb_rest(n_)
                    sb_colsum(n_)
                rstd_sq([Oacc[:]], 256, sc1[:, 4:5], ["Oacc"], "rstd_b", 4)
                Sc.op("dve", "tensor_scalar", reads=["Oacc", "rstd_b"], writes=["mixed"], out=mixed[:, 384:640],
                      in0=Oacc[:], scalar1=sc1[:, 4:5], scalar2=None, op0=ALU.mult)

                stage(9)
                for c in range(8):
                    transpose_bf(mixed[:, c * 128:(c + 1) * 128], 128, 4, c, ["mixed"])
                Sc.op("dve", "tensor_tensor", reads=[BK[4], "gcols"], writes=["mixedT"], out=mixedT[:],
                      in0=bankbf(4).rearrange("p (c t) -> p c t", c=8), in1=bc_last(gcols[:, gb + 16:gb + 24], 128),
                      op=ALU.mult)
                if i + 1 < nblk:
                    stats_transpose(i + 1, gb + 0)
                for half in range(2):
                    for c in range(8):
                        Sc.op("pe", "matmul", reads=["mixedT", "w_out"], writes=[BK[5 + half]], out=bank(5 + half),
                              lhsT=mixedT[:, c, :], rhs=W_OUT[:, c, half * 512:(half + 1) * 512],
                              start=(c == 0), stop=(c == 7))
                rstd_sq([bank(5), bank(6)], 1024, sc1[:, 5:6], [BK[5], BK[6]], "rstd_y", 6)
                residual_out([5, 6], sc1[:, 5:6], "rstd_y", i)

            stage(10)
            prs = []
            for c in range(8):
                for hh in range(2):
                    prs.append((W_UP[:, c, hh * 2048:(hh + 1) * 2048],
                                w_up[l, c * 128:(c + 1) * 128, hh * 2048:(hh + 1) * 2048]))
            for c4 in range(4):
                prs.append((W_DN[:, c4 * 8:(c4 + 1) * 8, :],
                            w_down[l, c4 * 1024:(c4 + 1) * 1024, :].rearrange("(c p) n -> p c n", p=128)))
            Sc.dma_multi("pool", "wld", prs, writes=BIGKEYS)
            Sc.dma("sp", "gbc", gbc[:], gbc_in[l * 2 + 1], writes=["gbc"])
            nblk_f = nblk if not DBG.get('skip_ffn') else 0
            if nblk_f:
                load_x(y_out, 0, False, t0row=2 * l + 1)
            for i in range(nblk_f):
                if i + 1 < nblk_f:
                    load_x(y_out, i + 1, False)
                if i == 0:
                    stats_transpose(i, gb + 8, want_s1=True)

                def up(fb):
                    b = 2 + fb % 4
                    for f4 in range(4):
                        fc = fb * 4 + f4
                        for c in range(8):
                            Sc.op("pe", "matmul", reads=["xgT", "w_up"], writes=[BK[b]], out=bank(b, 128, f4 * 128),
                                  lhsT=W_UP[:, c, fc * 128:(fc + 1) * 128], rhs=xgT[:, c, :],
                                  start=(c == 0), stop=(c == 7))
                    rs = fb % 2
                    Sc.op("dve", "tensor_scalar", reads=[BK[b]], writes=["rtmp%d" % rs], out=rtmp[:, rs, :], in0=bank(b),
                          scalar1=0.0, scalar2=None, op0=ALU.max)
                    Sc.op("pool", "tensor_tensor", reads=["rtmp%d" % rs], writes=["hT%d" % fb],
                          out=hT[:, fb * 4:(fb + 1) * 4, :].rearrange("p a t -> p (a t)"), in0=rtmp[:, rs, :],
                          in1=rtmp[:, rs, :], op=ALU.mult)

                def down(fb):
                    for f4 in range(4):
                        fc = fb * 4 + f4
                        for half in range(2):
                            Sc.op("pe", "matmul", reads=["hT%d" % fb, "w_dn"], writes=[BK[6 + half]], out=bank(6 + half),
                                  lhsT=hT[:, fc, :], rhs=W_DN[:, fc, half * 512:(half + 1) * 512],
                                  start=(fc == 0), stop=(fc == 31))

                up(0)
                for fb in range(8):
                    if fb + 1 < 8:
                        up(fb + 1)
                    elif i + 1 < nblk_f:
                        stats_transpose(i + 1, gb + 8, want_s1=True)
                    down(fb)
                s1k = "s1_%d" % (i % 2)
                s1ap = sc1[:, 6 + i % 2:7 + i % 2]
                rstd_sq([bank(6), bank(7)], 1024, sc1[:, 5:6], [BK[6], BK[7]], "rstd_y", 8,
                        scale_ap=ssq[:, 12 + i % 2:13 + i % 2], scale_key="s1q_%d" % (i % 2))
                Sc.op("dve", "tensor_tensor", reads=["rstd_y", s1k], writes=["rstd_y"], out=sc1[:, 5:6], in0=sc1[:, 5:6],
                      in1=s1ap, op=ALU.mult)
                residual_out([6, 7], sc1[:, 5:6], "rstd_y", i)

        if nlayers > 0 and not Sc.stopped:
            Sc.dma("sp", "t0fin", ytmp[0:1, :], t0tab[2 * nlayers:2 * nlayers + 1, :], reads=["t0tab%d" % (2 * nlayers)],
                   writes=["ytmp"])
            Sc.dma("sp", "t0fin", y_out[0:1, :], ytmp[0:1, :], reads=["ytmp", "xsrc0"], writes=["xsrc0"])
        Sc.wait_all("sp", ["xsrc%d" % i for i in range(nblk)])
        Sc.emit()
    return nc


def prep_inputs(x, positions, w_in, w_out, sinks, branch_gain, w_up, w_down,
                norm_mix_pre, norm_mix_post, norm_mlp_pre, norm_mlp_post):
    f = lambda a: np.ascontiguousarray(np.asarray(a, dtype=np.float32))
    cols = np.zeros((128, DEPTH, 3, 8), np.float32)
    for l in range(DEPTH):
        for k, g in enumerate((norm_mix_pre, norm_mlp_pre, branch_gain)):
            cols[:, l, k, :] = np.asarray(g, np.float32)[l].reshape(8, 128).T
    gb = np.zeros((DEPTH, 2, 128, D), np.float32)
    for l in range(DEPTH):
        gb[l, 0] = np.broadcast_to(np.asarray(norm_mix_post, np.float32)[l][None, :], (128, D))
        gb[l, 1] = np.broadcast_to(np.asarray(norm_mlp_post, np.float32)[l][None, :], (128, D))
    gr = np.zeros((DEPTH, 5, D), np.float32)
    for l in range(DEPTH):
        for k, g in enumerate((norm_mix_pre, branch_gain, norm_mix_post, norm_mlp_pre, norm_mlp_post)):
            gr[l, k] = np.asarray(g, np.float32)[l]
    shared = {
        "grows": np.ascontiguousarray(gr.reshape(DEPTH, 5 * D)),
        "w_in": f(w_in), "w_out": f(w_out), "w_up": f(w_up), "w_down": f(w_down),
        "consts": make_consts(),
        "gcols": np.ascontiguousarray(cols.reshape(128, DEPTH * 24)),
        "sinksb": np.ascontiguousarray(np.broadcast_to(np.asarray(sinks, np.float32).reshape(1, DEPTH * 6), (128, DEPTH * 6))),
        "gbc": np.ascontiguousarray(gb.reshape(DEPTH * 2, 128, D)),
        "pos": np.ascontiguousarray(np.asarray(positions, dtype=np.int32).reshape(64, 128)),
    }
    return shared


_NC_CACHE = {}


def kernel(x, positions, w_in, w_out, sinks, branch_gain, w_up, w_down,
           norm_mix_pre, norm_mix_post, norm_mlp_pre, norm_mlp_post):
    x = np.asarray(x, dtype=np.float32)
    shared = prep_inputs(x, positions, w_in, w_out, sinks, branch_gain, w_up, w_down,
                         norm_mix_pre, norm_mix_post, norm_mlp_pre, norm_mlp_post)
    if "nc" not in _NC_CACHE:
        _NC_CACHE["nc"] = build_nc()
    nc = _NC_CACHE["nc"]
    in_maps = []
    for b in range(8):
        m = dict(shared)
        m["x"] = np.ascontiguousarray(x[b])
        in_maps.append(m)
    res = run_bass_kernel_spmd(nc, in_maps, core_ids=list(range(8)))
    return np.stack([np.asarray(r["y"], dtype=np.float32) for r in res.results], axis=0)
```

```python
from contextlib import ExitStack
import math
import numpy as np
import concourse.bass as bass
import concourse.mybir as mybir
from concourse.bass_utils import run_bass_kernel_spmd

F32 = mybir.dt.float32
BF16 = mybir.dt.bfloat16
I32 = mybir.dt.int32
AF = mybir.ActivationFunctionType
ALU = mybir.AluOpType

D = 1024
S_FULL = 8192
DEPTH = 4
IN_W = 2944
DFF = 4096
EPS = 1e-6
EPOCH = 30000
DBG = {}
LOG_GAMMA = [math.log1p(-(2.0 ** (-5.0 - h))) for h in range(6)]

C_ID = 0
C_UI = 128
C_MS = 256
C_MC = 384
C_MP = 512
C_DD = 640
C_QD = 1408
C_KD = 1414
C_IF = 1420
C_ONE = 1452
NCON = 1456


def make_consts():
    c = np.zeros((128, NCON), np.float64)
    i = np.arange(128)
    c[:, C_ID:C_ID + 128] = np.eye(128)
    c[:, C_UI:C_UI + 128] = (i[:, None] >= i[None, :])
    c[:, C_MS:C_MS + 128] = (i[:, None] < i[None, :])
    c[:, C_MC:C_MC + 128] = (i[:, None] <= i[None, :])
    c[:, C_MP:C_MP + 128] = (i[:, None] > i[None, :])
    lg = np.array([np.log1p(-np.float32(2.0) ** np.float32(-5.0 - h)) for h in range(6)], np.float32).astype(np.float64)
    rel = i[None, :] - i[:, None]
    for h in range(6):
        dd = np.where(rel >= 0, np.exp(np.maximum(rel, 0) * lg[h]), 0.0) * 0.125
        c[:, C_DD + h * 128:C_DD + (h + 1) * 128] = dd
        c[:, C_QD + h] = np.exp((i + 1.0) * lg[h])
        c[:, C_KD + h] = np.exp((127.0 - i) * lg[h]) * 0.125
    invf = (np.float32(10000.0) ** (-(np.arange(0, 64, 2, dtype=np.float32)) / np.float32(64))).astype(np.float64)
    c[:, C_IF:C_IF + 32] = invf[None, :]
    c[:, C_ONE:C_ONE + 2] = 1.0
    return c.astype(np.float32)


class Sched:
    def __init__(self, nc, stack):
        self.nc = nc
        self.stack = stack
        self.eng_names = ["pe", "act", "dve", "pool", "sp"]
        self.ops = {e: [] for e in self.eng_names}
        self.count = {e: 0 for e in self.eng_names}
        self.prog_sems = {e: [] for e in self.eng_names}
        self.dma_sems = {}
        self.last_w = {}
        self.readers = {}
        self.known = {e: {} for e in self.eng_names}
        self.sem_owner = {}
        self.stopped = False

    def _new_sem(self, name, owner):
        s = self.stack.enter_context(self.nc.semaphore(name))
        self.sem_owner[id(s)] = owner
        return s

    def _prog_token(self, e):
        k = self.count[e]
        ep, off = divmod(k, EPOCH)
        while len(self.prog_sems[e]) <= ep:
            self.prog_sems[e].append(self._new_sem(f"p_{e}_{len(self.prog_sems[e])}", e))
        self.count[e] = k + 1
        return (self.prog_sems[e][ep], off + 1)

    def _deps(self, e, reads, writes):
        best = {}

        def add(t):
            s, v = t
            if e == "pe" and self.sem_owner[id(s)] == "pe":
                return
            cur = best.get(id(s))
            if cur is None or v > cur[1]:
                best[id(s)] = (s, v)

        for r in reads:
            t = self.last_w.get(r)
            if t is not None:
                add(t)
            if len(r) == 2 and r[0] == "B":
                for t in self.readers.get(r, {}).values():
                    if self.sem_owner[id(t[0])] != e:
                        add(t)
        for w in writes:
            t = self.last_w.get(w)
            if t is not None:
                add(t)
            for t in self.readers.get(w, {}).values():
                add(t)
        waits = []
        kn = self.known[e]
        for sid, (s, v) in best.items():
            if kn.get(sid, 0) >= v:
                continue
            kn[sid] = v
            waits.append((s, v))
        return waits

    def _commit(self, tok, reads, writes):
        for r in reads:
            d = self.readers.setdefault(r, {})
            cur = d.get(id(tok[0]))
            if cur is None or tok[1] > cur[1]:
                d[id(tok[0])] = tok
        for w in writes:
            self.last_w[w] = tok
            self.readers[w] = {}

    def op(self, e, meth, reads=(), writes=(), **kw):
        if self.stopped:
            return
        waits = self._deps(e, reads, writes)
        tok = self._prog_token(e)
        self.ops[e].append((meth, kw, waits, tok, 1))
        self._commit(tok, reads, writes)

    def dma(self, e, semkey, out, in_, reads=(), writes=()):
        if self.stopped:
            return
        waits = self._deps(e, reads, writes)
        ent = self.dma_sems.get(semkey)
        if ent is None:
            ent = [self._new_sem(f"d_{len(self.dma_sems)}", "dma"), 0]
            self.dma_sems[semkey] = ent
        ent[1] += 16
        tok = (ent[0], ent[1])
        self.ops[e].append(("dma_start", dict(out=out, in_=in_), waits, tok, 16))
        self._commit(tok, reads, writes)

    def dma_multi(self, e, semkey, pairs, reads=(), writes=()):
        if self.stopped:
            return
        waits = self._deps(e, reads, writes)
        ent = self.dma_sems.get(semkey)
        if ent is None:
            ent = [self._new_sem(f"d_{len(self.dma_sems)}", "dma"), 0]
            self.dma_sems[semkey] = ent
        tok = None
        for n, (out, in_) in enumerate(pairs):
            ent[1] += 16
            tok = (ent[0], ent[1])
            self.ops[e].append(("dma_start", dict(out=out, in_=in_), waits if n == 0 else [], tok, 16))
        self._commit(tok, reads, writes)

    def wait_all(self, e, keys):
        waits = self._deps(e, list(keys), [])
        self.ops[e].append((None, None, waits, None, 0))

    def emit(self):
        nc = self.nc
        handles = {"pe": "tensor", "act": "scalar", "dve": "vector", "pool": "gpsimd", "sp": "sync"}
        with nc.Block() as block:
            for e in self.eng_names:
                ops = self.ops[e]
                if not ops:
                    continue

                def body(eng, ops=ops):
                    for (meth, kw, waits, tok, inc) in ops:
                        for (s, v) in waits:
                            eng.wait_ge(s, v)
                        if meth is not None:
                            ins = getattr(eng, meth)(**kw)
                            ins.then_inc(tok[0], inc)

                getattr(block, handles[e])(body)


def build_nc(nlayers=DEPTH, nblk=64, stop=None):
    S = nblk * 128
    nc = bass.Bass("TRN2", target_bir_lowering=False)
    x_in = nc.dram_tensor("x", [S, D], F32, kind="ExternalInput").ap()
    pos_in = nc.dram_tensor("pos", [64, 128], I32, kind="ExternalInput").ap()
    w_in = nc.dram_tensor("w_in", [DEPTH, D, IN_W], F32, kind="ExternalInput").ap()
    w_out = nc.dram_tensor("w_out", [DEPTH, D, D], F32, kind="ExternalInput").ap()
    w_up = nc.dram_tensor("w_up", [DEPTH, D, DFF], F32, kind="ExternalInput").ap()
    w_down = nc.dram_tensor("w_down", [DEPTH, DFF, D], F32, kind="ExternalInput").ap()
    consts_in = nc.dram_tensor("consts", [128, NCON], F32, kind="ExternalInput").ap()
    gcols_in = nc.dram_tensor("gcols", [128, DEPTH * 3 * 8], F32, kind="ExternalInput").ap()
    sinks_in = nc.dram_tensor("sinksb", [128, DEPTH * 6], F32, kind="ExternalInput").ap()
    gbc_in = nc.dram_tensor("gbc", [DEPTH * 2, 128, D], F32, kind="ExternalInput").ap()
    grows_in = nc.dram_tensor("grows", [DEPTH, 5 * D], F32, kind="ExternalInput").ap()
    y_out = nc.dram_tensor("y", [S, D], F32, kind="ExternalOutput").ap()
    t0tab = nc.dram_tensor("t0tab", [2 * DEPTH + 1, D], F32, kind="Internal").ap()
    cstab = nc.dram_tensor("cstab", [64 * 128, 128], F32, kind="Internal").ap()

    with ExitStack() as st:
        Sc = Sched(nc, st)

        def sb(name, shape, dt):
            return st.enter_context(nc.sbuf_tensor("s_" + name, shape, dt))

        ps = st.enter_context(nc.psum_tensor("ps", [128, 4096], F32))

        def bank(b, w=512, off=0):
            return ps[:, b * 512 + off:b * 512 + off + w]

        def bankbf(b):
            return ps[:, b * 512:(b + 1) * 512].bitcast(BF16)

        BK = [f"B{b}" for b in range(8)]

        big = sb("big", [128, 65536], BF16)
        cst = sb("cst", [128, NCON], F32)
        identb = sb("identb", [128, 128], BF16)
        uinclb = sb("uinclb", [128, 128], BF16)
        mstrb = sb("mstrb", [128, 128], BF16)
        mcurb = sb("mcurb", [128, 128], BF16)
        mprevb = sb("mprevb", [128, 128], BF16)
        onesb = sb("onesb", [128, 2], BF16)
        gcols = sb("gcols", [128, DEPTH * 3 * 8], F32)
        esink = sb("esink", [128, DEPTH * 6], F32)
        gbc = sb("gbc", [128, D], F32)
        xtb = sb("xt", [128, 2, D], F32)
        ytmp = sb("ytmp", [128, D], F32)
        xgT = sb("xgT", [128, 8, 128], BF16)
        csb = sb("cs", [128, 2, 128], F32)
        stt = sb("stt", [128, 12], F32)
        mv = sb("mv", [128, 2], F32)
        sc1 = sb("sc1", [128, 8], F32)
        U = sb("U", [128, 3072], F32)
        hT = U[:, 0:2048].bitcast(BF16).rearrange("p (f t) -> p f t", f=32)
        rtmp = U[:, 2048:3072].rearrange("p (r n) -> p r n", r=2)
        Et = U[:, 0:1024]
        ropA = U[:, 1024:1792]
        ropB = U[:, 1792:2560]
        onesq = sb("onesq", [128, 128], BF16)
        qbTp = sb("qbTp", [128, 4, 128], BF16)
        qa_p = sb("qa_p", [128, 384], BF16)
        ka_r = sb("ka_r", [128, 128], BF16)
        va_e = sb("va_e", [128, 2, 2, 66], BF16)
        qb_s = sb("qb_s", [128, 256], BF16)
        kb_n = sb("kb_n", [128, 256], BF16)
        qc_r = sb("qc_r", [128, 384], BF16)
        qcd = sb("qcd", [128, 384], BF16)
        kc_r = sb("kc_r", [128, 384], BF16)
        kdk = sb("kdk", [128, 384], BF16)
        vc_s = sb("vc_s", [128, 384], BF16)
        sil = sb("sil", [128, 384], F32)
        silt = sb("silt", [128, 384], F32)
        qaT = sb("qaT", [128, 3, 128], BF16)
        kaT = sb("kaT", [128, 2, 128], BF16)
        qcT = sb("qcT", [128, 3, 128], BF16)
        kcT = sb("kcT", [128, 3, 128], BF16)
        qcdT = sb("qcdT", [64, 6, 128], BF16)
        Lt = sb("Lt", [128, 2, 1024], BF16)
        Wt = sb("Wt", [128, 2, 1024], BF16)
        Oacc = sb("Oacc", [128, 256], F32)
        gt = sb("gt", [128, 2, 4], F32)
        PT = sb("PT", [128, 2, 768], BF16)
        oa = sb("oa", [128, 384], F32)
        den = sb("den", [128, 6], F32)
        intraT = sb("intraT", [128, 768], BF16)
        state = sb("state", [64, 384], F32)
        stateb = sb("stateb", [64, 384], BF16)
        gst = sb("gst", [128, 36], F32)
        gmv = sb("gmv", [128, 12], F32)
        grs = sb("grs", [128, 6], F32)
        oc = sb("oc", [128, 384], F32)
        mixed = sb("mixed", [128, D], BF16)
        mixedT = sb("mixedT", [128, 8, 128], BF16)

        W_IN = big[:, 0:8 * IN_W].rearrange("p (c n) -> p c n", c=8)
        o1 = 8 * IN_W
        W_OUT = big[:, o1:o1 + 8192].rearrange("p (c n) -> p c n", c=8)
        o2 = o1 + 8192
        KBT = big[:, o2:o2 + 16384].rearrange("p (c n) -> p c n", c=2)
        o3 = o2 + 16384
        VC = big[:, o3:o3 + 16384].rearrange("p (b n) -> p b n", b=64)
        W_UP = big[:, 0:32768].rearrange("p (c n) -> p c n", c=8)
        W_DN = big[:, 32768:65536].rearrange("p (c n) -> p c n", c=32)
        T0KEYS = ["t0x", "t0xg", "t0p", "t0m", "t0h", "t0y", "t0g", "t0s", "t0w1", "t0w2", "t0c", "t0sa", "t0sb", "t0sc"]
        BIGKEYS = ["w_in", "w_out", "kbT", "vcache", "w_up", "w_dn", "startup", "w32", "x32", "qk32"] + T0KEYS
        W32 = big[:, o3:o3 + 12288].bitcast(F32).rearrange("p (c n) -> p c n", c=8)
        X32 = big[:, o2 + 1024:o2 + 3072].bitcast(F32).rearrange("p (c t) -> p c t", c=8)
        QK32 = big[0:64, o2 + 3072:o2 + 6144].bitcast(F32).rearrange("p (h t) -> p h t", h=12)

        def bc_last(ap, n):
            sh = list(ap.shape)
            return ap.unsqueeze(len(sh)).to_broadcast(sh + [n])

        def bc_mid(ap, n):
            sh = list(ap.shape)
            return ap.unsqueeze(1).to_broadcast([sh[0], n] + sh[1:])

        Sc.dma("sp", "cst", cst[:], consts_in[:, :], writes=["cst"])
        Sc.dma("sp", "gcols", gcols[:], gcols_in[:, :], writes=["gcols"])
        Sc.dma("sp", "esink", esink[:], sinks_in[:, :], writes=["esink"])
        for (dst, col, nm) in [(identb, C_ID, "identb"), (uinclb, C_UI, "uinclb"), (mstrb, C_MS, "mstrb"),
                               (mcurb, C_MC, "mcurb"), (mprevb, C_MP, "mprevb")]:
            Sc.op("pool", "tensor_copy", reads=["cst"], writes=[nm], out=dst[:], in_=cst[:, col:col + 128])
        Sc.op("pool", "tensor_copy", reads=["cst"], writes=["onesb"], out=onesb[:], in_=cst[:, C_ONE:C_ONE + 2])
        Sc.op("pool", "memset", writes=["va_e0", "va_e1"], ap=va_e[:], constant=1.0)
        Sc.op("pool", "memset", writes=["onesq"], ap=onesq[:], constant=1.0)
        Sc.op("pool", "memset", writes=["qbTp"], ap=qbTp[:], constant=0.0)
        Sc.op("act", "activation", reads=["esink"], writes=["esink"], out=esink[:], in_=esink[:], func=AF.Exp)

        bigf = big[:].bitcast(F32)
        posi = sb("posi", [64, 128], I32)
        posf = sb("posf", [64, 128], F32)
        posT = sb("posT", [128, 64], F32)
        ang = bigf[:, 0:4096].rearrange("p (n j i) -> p n j i", n=64, j=2)
        tq = bigf[:, 4096:8192].rearrange("p (n j i) -> p n j i", n=64, j=2)
        tki = bigf[:, 8192:12288].bitcast(I32).rearrange("p (n j i) -> p n j i", n=64, j=2)
        tkf = bigf[:, 12288:16384].rearrange("p (n j i) -> p n j i", n=64, j=2)
        tab = bigf[:, 16384:24576].rearrange("p (n c) -> p n c", n=64)
        Sc.dma("sp", "posi", posi[:], pos_in[:, :], writes=["posi"])
        Sc.op("dve", "tensor_copy", reads=["posi"], writes=["posf"], out=posf[:], in_=posi[:])
        Sc.op("pe", "transpose", reads=["posf", "cst"], writes=[BK[0]], out=bank(0, 64), in_=posf[:],
              identity=cst[0:64, C_ID:C_ID + 64])
        Sc.op("dve", "tensor_copy", reads=[BK[0]], writes=["posT"], out=posT[:], in_=bank(0, 64))
        invf = cst[:, C_IF:C_IF + 32]
        Sc.op("dve", "tensor_tensor", reads=["posT", "cst"], writes=["startup"], out=ang[:, :, 0, :],
              in0=bc_last(posT[:], 32), in1=bc_mid(invf, 64), op=ALU.mult)
        Sc.op("dve", "tensor_scalar", reads=["startup"], writes=["startup"], out=ang[:, :, 1, :],
              in0=ang[:, :, 0, :], scalar1=math.pi / 2, scalar2=None, op0=ALU.add)
        TWO_PI = 2 * math.pi
        C1 = 6.28125
        C2 = float(np.float32(TWO_PI - C1))
        C3 = TWO_PI - C1 - C2
        Sc.op("dve", "tensor_scalar", reads=["startup"], writes=["startup"], out=tq, in0=ang,
              scalar1=1.0 / TWO_PI, scalar2=0.5, op0=ALU.mult, op1=ALU.add)
        Sc.op("dve", "tensor_copy", reads=["startup"], writes=["startup"], out=tki, in_=tq)
        Sc.op("dve", "tensor_copy", reads=["startup"], writes=["startup"], out=tkf, in_=tki)
        for cc in (C1, C2, C3):
            Sc.op("dve", "scalar_tensor_tensor", reads=["startup"], writes=["startup"], out=ang, in0=tkf,
                  scalar=-cc, in1=ang, op0=ALU.mult, op1=ALU.add)
        Sc.op("dve", "tensor_single_scalar", reads=["startup"], writes=["startup"], out=tq, in_=ang,
              scalar=math.pi, op=ALU.is_gt)
        Sc.op("dve", "scalar_tensor_tensor", reads=["startup"], writes=["startup"], out=ang, in0=tq,
              scalar=-TWO_PI, in1=ang, op0=ALU.mult, op1=ALU.add)
        Sc.op("dve", "tensor_single_scalar", reads=["startup"], writes=["startup"], out=tq, in_=ang,
              scalar=-math.pi, op=ALU.is_lt)
        Sc.op("dve", "scalar_tensor_tensor", reads=["startup"], writes=["startup"], out=ang, in0=tq,
              scalar=TWO_PI, in1=ang, op0=ALU.mult, op1=ALU.add)
        PI_S = 3.1415925
        Sc.op("dve", "tensor_scalar", reads=["startup"], writes=["startup"], out=ang, in0=ang,
              scalar1=PI_S, scalar2=-PI_S, op0=ALU.min, op1=ALU.max)
        Sc.op("act", "activation", reads=["startup"], writes=["startup"], out=tkf, in_=ang, func=AF.Sin)
        Sc.op("dve", "tensor_copy", reads=["startup"], writes=["startup"], out=tab[:, :, 0:32], in_=tkf[:, :, 1, :])
        Sc.op("dve", "tensor_copy", reads=["startup"], writes=["startup"], out=tab[:, :, 32:64], in_=tkf[:, :, 1, :])
        Sc.op("dve", "tensor_scalar", reads=["startup"], writes=["startup"], out=tab[:, :, 64:96],
              in0=tkf[:, :, 0, :], scalar1=-1.0, scalar2=None, op0=ALU.mult)
        Sc.op("dve", "tensor_copy", reads=["startup"], writes=["startup"], out=tab[:, :, 96:128], in_=tkf[:, :, 0, :])
        cst_v = cstab.rearrange("(n p) c -> p n c", p=128)
        Sc.dma_multi("sp", "cstab", [(cst_v[:, 4 * q:4 * q + 4, :], tab[:, 4 * q:4 * q + 4, :]) for q in range(16)],
                     reads=["startup"], writes=["cstab"])


        def t0_path():
            AX = mybir.AxisListType.X
            bf = big[:].bitcast(F32)
            STG = [bf[:, 0:4096], bf[:, 4096:8192], bf[:, 8192:12288]]
            SK = ["t0sa", "t0sb", "t0sc"]
            off = [12288]

            def row(n):
                a = bf[0:1, off[0]:off[0] + n]
                off[0] += n
                return a

            t_x, t_xg, t_proj, t_mix, t_h, t_y = row(1024), row(1024), row(2944), row(1024), row(4096), row(1024)
            t_g, t_s, t_w1, t_w2 = row(5 * 1024), row(64), row(1024), row(1024)
            cb = off[0]
            t_xT = bf[:, cb:cb + 8]
            t_hT = bf[:, cb + 8:cb + 40]
            one2 = cst[0:1, C_ONE:C_ONE + 2]

            def sc(a, b):
                return t_s[:, a:b]

            def rstd_of(src, n, slot, keys):
                Sc.op("dve", "tensor_tensor", reads=keys, writes=["t0w2"], out=t_w2[:, 0:n], in0=src, in1=src, op=ALU.mult)
                Sc.op("dve", "tensor_reduce", reads=["t0w2"], writes=["t0s"], out=sc(slot, slot + 1), in_=t_w2[:, 0:n],
                      axis=AX, op=ALU.add)
                Sc.op("act", "activation", reads=["t0s"], writes=["t0s"], out=sc(slot, slot + 1), in_=sc(slot, slot + 1),
                      func=AF.Ln, bias=EPS, scale=1.0 / n)
                Sc.op("act", "activation", reads=["t0s"], writes=["t0s"], out=sc(slot, slot + 1), in_=sc(slot, slot + 1),
                      func=AF.Exp, scale=-0.5)

            def to_cols(src, nch, dst, keys):
                for c in range(nch):
                    Sc.op("pe", "matmul", reads=keys + ["cst"], writes=[BK[6]], out=ps[:, 6 * 512 + 2 * c:6 * 512 + 2 * c + 2],
                          lhsT=src[:, c * 128:(c + 1) * 128], rhs=one2, start=True, stop=True)
                Sc.op("dve", "tensor_copy", reads=[BK[6]], writes=["t0c"], out=dst,
                      in_=ps[:, 6 * 512:6 * 512 + 2 * nch].rearrange("p (c k) -> p c k", k=2)[:, :, 0])

            gi_ctr = [0]

            def stage_load(dram_ap, kc, w):
                k = gi_ctr[0] % 3
                gi_ctr[0] += 1
                v = STG[k][:, 0:kc * w].rearrange("p (c n) -> p c n", c=kc)
                Sc.dma("sp", "t0stg%d" % k, v, dram_ap, writes=[SK[k]])
                return v, SK[k]

            Sc.dma("sp", "t0ldx", t_x, x_in[0:1, :], writes=BIGKEYS)
            for l in range(nlayers):
                Sc.dma("sp", "t0ld", t_g, grows_in[l:l + 1, :], writes=["t0g"])
                g_mixpre, g_branch, g_mixpost, g_mlppre, g_mlppost = [t_g[:, k * 1024:(k + 1) * 1024] for k in range(5)]
                rstd_of(t_x, 1024, 0, ["t0x"])
                Sc.op("dve", "tensor_tensor", reads=["t0x", "t0g"], writes=["t0xg"], out=t_xg, in0=t_x, in1=g_mixpre, op=ALU.mult)
                to_cols(t_xg, 8, t_xT, ["t0xg"])
                for gi in range(6):
                    w = 512 if gi < 5 else 384
                    v, vk = stage_load(w_in[l, :, gi * 512:gi * 512 + w].rearrange("(c p) n -> p c n", p=128), 8, w)
                    for c in range(8):
                        Sc.op("pe", "matmul", reads=["t0c", vk], writes=[BK[gi % 6]], out=ps[0:1, (gi % 6) * 512:(gi % 6) * 512 + w],
                              lhsT=t_xT[:, c:c + 1], rhs=v[:, c, :], start=(c == 0), stop=(c == 7))
                    Sc.op("dve", "tensor_scalar", reads=[BK[gi % 6], "t0s"], writes=["t0p"], out=t_proj[:, gi * 512:gi * 512 + w],
                          in0=ps[0:1, (gi % 6) * 512:(gi % 6) * 512 + w], scalar1=sc(0, 1), scalar2=None, op0=ALU.mult)
                qa4 = t_proj[:, 0:384].rearrange("p (g j d) -> p g j d", g=2, j=3)
                ka4 = t_proj[:, 384:512].rearrange("p (g d) -> p g d", g=2).unsqueeze(2).to_broadcast([1, 2, 3, 64])
                va4 = t_proj[:, 512:640].rearrange("p (g d) -> p g d", g=2).unsqueeze(2).to_broadcast([1, 2, 3, 64])
                w14 = t_w1[:, 0:384].rearrange("p (g j d) -> p g j d", g=2, j=3)
                w16 = t_w1[:, 0:384].rearrange("p (h d) -> p h d", h=6)
                w26 = t_w2[:, 0:384].rearrange("p (h d) -> p h d", h=6)
                Sc.op("dve", "tensor_tensor", reads=["t0p"], writes=["t0w1"], out=w14, in0=qa4, in1=ka4, op=ALU.mult)
                Sc.op("dve", "tensor_reduce", reads=["t0w1"], writes=["t0s"], out=sc(8, 14), in_=w16, axis=AX, op=ALU.add)
                Sc.op("act", "activation", reads=["t0s"], writes=["t0s"], out=sc(16, 22), in_=sc(8, 14), func=AF.Exp, scale=0.125)
                Sc.op("dve", "tensor_tensor", reads=["t0s", "esink"], writes=["t0s"], out=sc(48, 54), in0=sc(16, 22),
                      in1=esink[0:1, l * 6:(l + 1) * 6], op=ALU.add)
                Sc.op("dve", "reciprocal", reads=["t0s"], writes=["t0s"], out=sc(48, 54), in_=sc(48, 54))
                Sc.op("dve", "tensor_tensor", reads=["t0s"], writes=["t0s"], out=sc(16, 22), in0=sc(16, 22), in1=sc(48, 54), op=ALU.mult)
                p4 = sc(16, 22).rearrange("p (g j) -> p g j", g=2).unsqueeze(3).to_broadcast([1, 2, 3, 64])
                Sc.op("dve", "tensor_tensor", reads=["t0p", "t0s"], writes=["t0w1"], out=w14, in0=va4, in1=p4, op=ALU.mult)
                rstd_of(t_w1[:, 0:384], 384, 1, ["t0w1"])
                Sc.op("dve", "scalar_tensor_tensor", reads=["t0w1", "t0s", "t0g"], writes=["t0m"], out=t_mix[:, 0:384],
                      in0=t_w1[:, 0:384], scalar=sc(1, 2), in1=g_branch[:, 0:384], op0=ALU.mult, op1=ALU.mult)
                Sc.op("dve", "memset", writes=["t0m"], ap=t_mix[:, 384:640], constant=0.0)
                qc6 = t_proj[:, 1408:1792].rearrange("p (h d) -> p h d", h=6)
                kc6 = t_proj[:, 1792:2176].rearrange("p (h d) -> p h d", h=6)
                vc6 = t_proj[:, 2176:2560].rearrange("p (h d) -> p h d", h=6)
                gcr = t_proj[:, 2560:2944]
                Sc.op("dve", "tensor_tensor", reads=["t0p"], writes=["t0w1"], out=w16, in0=qc6, in1=kc6, op=ALU.mult)
                Sc.op("dve", "tensor_reduce", reads=["t0w1"], writes=["t0s"], out=sc(24, 30), in_=w16, axis=AX, op=ALU.add)
                Sc.op("dve", "tensor_scalar", reads=["t0s"], writes=["t0s"], out=sc(24, 30), in0=sc(24, 30), scalar1=0.125,
                      scalar2=None, op0=ALU.mult)
                Sc.op("dve", "tensor_tensor", reads=["t0p", "t0s"], writes=["t0w1"], out=w16, in0=vc6, in1=bc_last(sc(24, 30), 64),
                      op=ALU.mult)
                Sc.op("dve", "tensor_reduce", reads=["t0w1"], writes=["t0s"], out=sc(32, 38), in_=w16, axis=AX, op=ALU.add)
                Sc.op("dve", "tensor_scalar", reads=["t0s"], writes=["t0s"], out=sc(32, 38), in0=sc(32, 38), scalar1=1.0 / 64,
                      scalar2=None, op0=ALU.mult)
                Sc.op("dve", "tensor_tensor", reads=["t0w1", "t0s"], writes=["t0w1"], out=w16, in0=w16, in1=bc_last(sc(32, 38), 64),
                      op=ALU.subtract)
                Sc.op("dve", "tensor_tensor", reads=["t0w1"], writes=["t0w2"], out=w26, in0=w16, in1=w16, op=ALU.mult)
                Sc.op("dve", "tensor_reduce", reads=["t0w2"], writes=["t0s"], out=sc(40, 46), in_=w26, axis=AX, op=ALU.add)
                Sc.op("act", "activation", reads=["t0s"], writes=["t0s"], out=sc(40, 46), in_=sc(40, 46), func=AF.Ln, bias=EPS,
                      scale=1.0 / 64)
                Sc.op("act", "activation", reads=["t0s"], writes=["t0s"], out=sc(40, 46), in_=sc(40, 46), func=AF.Exp, scale=-0.5)
                Sc.op("dve", "tensor_tensor", reads=["t0w1", "t0s"], writes=["t0w1"], out=w16, in0=w16, in1=bc_last(sc(40, 46), 64),
                      op=ALU.mult)
                Sc.op("dve", "tensor_tensor", reads=["t0w1", "t0g"], writes=["t0w1"], out=t_w1[:, 0:384], in0=t_w1[:, 0:384],
                      in1=g_branch[:, 640:1024], op=ALU.mult)
                Sc.op("act", "activation", reads=["t0p"], writes=["t0w2"], out=t_w2[:, 0:384], in_=gcr, func=AF.Exp, scale=-1.0)
                Sc.op("dve", "tensor_scalar", reads=["t0w2"], writes=["t0w2"], out=t_w2[:, 0:384], in0=t_w2[:, 0:384], scalar1=1.0,
                      scalar2=None, op0=ALU.add)
                Sc.op("dve", "reciprocal", reads=["t0w2"], writes=["t0w2"], out=t_w2[:, 0:384], in_=t_w2[:, 0:384])
                Sc.op("dve", "tensor_tensor", reads=["t0w2", "t0p"], writes=["t0w2"], out=t_w2[:, 0:384], in0=t_w2[:, 0:384],
                      in1=gcr, op=ALU.mult)
                Sc.op("dve", "tensor_tensor", reads=["t0w1", "t0w2"], writes=["t0m"], out=t_mix[:, 640:1024], in0=t_w1[:, 0:384],
                      in1=t_w2[:, 0:384], op=ALU.mult)
                to_cols(t_mix, 8, t_xT, ["t0m"])
                for g in range(2):
                    v, vk = stage_load(w_out[l, :, g * 512:(g + 1) * 512].rearrange("(c p) n -> p c n", p=128), 8, 512)
                    for c in range(8):
                        Sc.op("pe", "matmul", reads=["t0c", vk], writes=[BK[g]], out=ps[0:1, g * 512:(g + 1) * 512],
                              lhsT=t_xT[:, c:c + 1], rhs=v[:, c, :], start=(c == 0), stop=(c == 7))
                    Sc.op("dve", "tensor_copy", reads=[BK[g]], writes=["t0y"], out=t_y[:, g * 512:(g + 1) * 512],
                          in_=ps[0:1, g * 512:(g + 1) * 512])
                rstd_of(t_y, 1024, 2, ["t0y"])
                Sc.op("dve", "scalar_tensor_tensor", reads=["t0y", "t0s", "t0g"], writes=["t0y"], out=t_y, in0=t_y, scalar=sc(2, 3),
                      in1=g_mixpost, op0=ALU.mult, op1=ALU.mult)
                Sc.op("dve", "tensor_tensor", reads=["t0y", "t0x"], writes=["t0x"], out=t_x, in0=t_x, in1=t_y, op=ALU.add)
                Sc.dma("sp", "t0st", t0tab[2 * l + 1:2 * l + 2, :], t_x, reads=["t0x"], writes=["t0tab%d" % (2 * l + 1)])
                rstd_of(t_x, 1024, 0, ["t0x"])
                Sc.op("dve", "tensor_tensor", reads=["t0x", "t0g"], writes=["t0xg"], out=t_xg, in0=t_x, in1=g_mlppre, op=ALU.mult)
                to_cols(t_xg, 8, t_xT, ["t0xg"])
                for g in range(8):
                    v, vk = stage_load(w_up[l, :, g * 512:(g + 1) * 512].rearrange("(c p) n -> p c n", p=128), 8, 512)
                    b = g % 6
                    for c in range(8):
                        Sc.op("pe", "matmul", reads=["t0c", vk], writes=[BK[b]], out=ps[0:1, b * 512:(b + 1) * 512],
                              lhsT=t_xT[:, c:c + 1], rhs=v[:, c, :], start=(c == 0), stop=(c == 7))
                    Sc.op("dve", "tensor_scalar", reads=[BK[b], "t0s"], writes=["t0w1"], out=t_w1[:, 0:512],
                          in0=ps[0:1, b * 512:(b + 1) * 512], scalar1=sc(0, 1), scalar2=0.0, op0=ALU.mult, op1=ALU.max)
                    Sc.op("dve", "tensor_tensor", reads=["t0w1"], writes=["t0h"], out=t_h[:, g * 512:(g + 1) * 512],
                          in0=t_w1[:, 0:512], in1=t_w1[:, 0:512], op=ALU.mult)
                to_cols(t_h, 32, t_hT, ["t0h"])
                for g in range(8):
                    v, vk = stage_load(w_down[l, g * 512:(g + 1) * 512, :].rearrange("(c p) n -> p c n", p=128), 4, 1024)
                    for k in range(4):
                        fc = g * 4 + k
                        for half in range(2):
                            Sc.op("pe", "matmul", reads=["t0c", vk], writes=[BK[half]], out=ps[0:1, half * 512:(half + 1) * 512],
                                  lhsT=t_hT[:, fc:fc + 1], rhs=v[:, k, half * 512:(half + 1) * 512],
                                  start=(fc == 0), stop=(fc == 31))
                for half in range(2):
                    Sc.op("dve", "tensor_copy", reads=[BK[half]], writes=["t0y"], out=t_y[:, half * 512:(half + 1) * 512],
                          in_=ps[0:1, half * 512:(half + 1) * 512])
                rstd_of(t_y, 1024, 2, ["t0y"])
                Sc.op("dve", "scalar_tensor_tensor", reads=["t0y", "t0s", "t0g"], writes=["t0y"], out=t_y, in0=t_y, scalar=sc(2, 3),
                      in1=g_mlppost, op0=ALU.mult, op1=ALU.mult)
                Sc.op("dve", "tensor_tensor", reads=["t0y", "t0x"], writes=["t0x"], out=t_x, in0=t_x, in1=t_y, op=ALU.add)
                Sc.dma("sp", "t0st", t0tab[2 * l + 2:2 * l + 3, :], t_x, reads=["t0x"], writes=["t0tab%d" % (2 * l + 2)])

        t0_path()
        def rstd_from_mv(mv_ap, dst, key_in, key_out, scale=1.0):
            Sc.op("dve", "scalar_tensor_tensor", reads=[key_in], writes=[key_out], out=dst, in0=mv_ap[:, 0:1],
                  scalar=mv_ap[:, 0:1], in1=mv_ap[:, 1:2], op0=ALU.mult, op1=ALU.add)
            Sc.op("act", "activation", reads=[key_out], writes=[key_out], out=dst, in_=dst, func=AF.Ln,
                  bias=EPS, scale=scale)
            Sc.op("act", "activation", reads=[key_out], writes=[key_out], out=dst, in_=dst, func=AF.Exp,
                  scale=-0.5)

        sqj = sb("sqj", [128, 512], BF16)
        ssq = sb("ssq", [128, 16], F32)

        def rstd_sq(srcs, n_tot, dst, keys_in, key_out, slot, scale_ap=None, scale_key=None):
            for k, ap in enumerate(srcs):
                Sc.op("act", "activation", reads=keys_in, writes=["sqj", "ssq%d_%d" % (slot, k)], out=sqj[:, 0:ap.shape[1]], in_=ap,
                      func=AF.Square, accum_out=ssq[:, slot + k:slot + k + 1])
            src = ssq[:, slot:slot + 1]
            rk = ["ssq%d_%d" % (slot, k) for k in range(len(srcs))]
            if len(srcs) == 2:
                Sc.op("pool", "tensor_tensor", reads=rk, writes=["ssq%d_0" % slot], out=src, in0=src,
                      in1=ssq[:, slot + 1:slot + 2], op=ALU.add)
                rk = ["ssq%d_0" % slot]
            if scale_ap is None:
                Sc.op("act", "activation", reads=rk, writes=[key_out], out=dst, in_=src, func=AF.Ln, bias=EPS,
                      scale=1.0 / n_tot)
            else:
                Sc.op("act", "activation", reads=rk + [scale_key], writes=[key_out], out=dst, in_=src, func=AF.Ln, bias=EPS,
                      scale=scale_ap)
            Sc.op("act", "activation", reads=[key_out], writes=[key_out], out=dst, in_=dst, func=AF.Exp, scale=-0.5)

        def row_stats(src_halves, key_src, n):
            for k, ap in enumerate(src_halves):
                Sc.op("dve", "bn_stats", reads=key_src, writes=["stt"], out=stt[:, 6 * k:6 * k + 6], in_=ap)
            Sc.op("dve", "bn_aggr", reads=["stt"], writes=["mv"], out=mv[:], in_=stt[:, 0:6 * n])

        def load_x(src_ap, i, with_cs, t0row=None):
            bf_ = i % 2
            xk = "xt%d" % bf_
            Sc.dma("sp", xk, xtb[:, bf_, :], src_ap[i * 128:(i + 1) * 128, :], reads=["xsrc%d" % i], writes=[xk])
            if t0row is not None:
                Sc.dma("sp", xk, xtb[0:1, bf_, :], t0tab[t0row:t0row + 1, :], reads=["t0tab%d" % t0row], writes=[xk])
            if with_cs:
                Sc.dma("sp", "cs%d" % bf_, csb[:, bf_, :], cstab[i * 128:(i + 1) * 128, :], reads=["cstab"],
                       writes=["cs%d" % bf_])

        def stats_transpose(i, gcol_base, x32=False, want_s1=False):
            bf_ = i % 2
            xk = "xt%d" % bf_
            xt = xtb[:, bf_, :]
            rstd_sq([xt[:, 0:512], xt[:, 512:1024]], 1024, sc1[:, 0:1], [xk], "rstd", 0)
            if want_s1:
                Sc.op("dve", "tensor_tensor", reads=["rstd"], writes=["s1_%d" % bf_], out=sc1[:, 6 + bf_:7 + bf_],
                      in0=sc1[:, 0:1], in1=sc1[:, 0:1], op=ALU.mult)
                Sc.op("dve", "scalar_tensor_tensor", reads=["s1_%d" % bf_], writes=["s1q_%d" % bf_], out=ssq[:, 12 + bf_:13 + bf_],
                      in0=sc1[:, 6 + bf_:7 + bf_], scalar=1.0 / 1024, in1=sc1[:, 6 + bf_:7 + bf_], op0=ALU.mult, op1=ALU.mult)
            for half in range(2):
                b = half
                for c4 in range(4):
                    c = half * 4 + c4
                    Sc.op("pe", "transpose", reads=[xk, "cst"], writes=[BK[b]], out=bank(b, 128, c4 * 128),
                          in_=xt[:, c * 128:(c + 1) * 128], identity=cst[:, C_ID:C_ID + 128])
                Sc.op("dve", "tensor_tensor", reads=[BK[b], "gcols"], writes=["xgT"], out=xgT[:, half * 4:(half + 1) * 4, :],
                      in0=bank(b).rearrange("p (c t) -> p c t", c=4),
                      in1=bc_last(gcols[:, gcol_base + half * 4:gcol_base + half * 4 + 4], 128), op=ALU.mult)
                for c4 in range(4):
                    c = half * 4 + c4
                    if x32:
                        Sc.op("dve", "tensor_scalar", reads=[BK[b], "gcols"], writes=["x32"], out=X32[:, c, :],
                              in0=bank(b, 128, c4 * 128), scalar1=gcols[:, gcol_base + c:gcol_base + c + 1],
                              scalar2=None, op0=ALU.mult)

        def residual_out(ybanks, scale_ap, scale_key, i):
            bf_ = i % 2
            xk = "xt%d" % bf_
            xt = xtb[:, bf_, :]
            for half, b in enumerate(ybanks):
                Sc.op("dve", "scalar_tensor_tensor", reads=[BK[b], scale_key, "gbc"], writes=["ytmp"],
                      out=ytmp[:, half * 512:(half + 1) * 512], in0=bank(b), scalar=scale_ap,
                      in1=gbc[:, half * 512:(half + 1) * 512], op0=ALU.mult, op1=ALU.mult)
            Sc.op("pool", "tensor_tensor", reads=["ytmp", xk], writes=[xk], out=xt, in0=xt, in1=ytmp[:],
                  op=ALU.add)
            Sc.dma("sp", "xst%d" % bf_, y_out[i * 128:(i + 1) * 128, :], xt, reads=[xk], writes=["xsrc%d" % i])

        def transpose_bf(src_ap, ncols, b, slot, rd):
            Sc.op("pe", "transpose", reads=rd + ["identb"], writes=[BK[b]],
                  out=bankbf(b)[0:ncols, slot * 128:(slot + 1) * 128], in_=src_ap, identity=identb[:])

        def stage(k):
            if stop is not None and k >= stop:
                Sc.stopped = True

        for l in range(nlayers):
            x_src = x_in if l == 0 else y_out
            prs = []
            for c in range(8):
                for hh in range(2):
                    prs.append((W_IN[:, c, hh * 1472:(hh + 1) * 1472],
                                w_in[l, c * 128:(c + 1) * 128, hh * 1472:(hh + 1) * 1472]))
            for c in range(8):
                prs.append((W_OUT[:, c, :], w_out[l, c * 128:(c + 1) * 128, :]))
            Sc.dma_multi("pool", "wld", prs, writes=BIGKEYS)
            Sc.dma("sp", "w32", W32, w_in[l, :, 1408:2176].rearrange("(c p) n -> p c n", p=128), writes=BIGKEYS)
            Sc.dma("sp", "gbc", gbc[:], gbc_in[l * 2 + 0], writes=["gbc"])
            Sc.op("pool", "memset", writes=["state"], ap=state[:], constant=0.0)
            gb = l * 24
            es = esink[:, l * 6:(l + 1) * 6]

            load_x(x_src, 0, True, t0row=(2 * l if l > 0 else None))
            for i in range(nblk):
                r = i % 2
                cs = csb[:, r, :]
                csk = "cs%d" % r
                if i + 1 < nblk:
                    load_x(x_src, i + 1, True)
                if i == 0:
                    stats_transpose(i, gb + 0, x32=True)
                stage(1)
                for b6 in range(6):
                    w = 512 if b6 < 5 else 384
                    for c in range(8):
                        Sc.op("pe", "matmul", reads=["xgT", "w_in"], writes=[BK[2 + b6]], out=bank(2 + b6, w),
                              lhsT=xgT[:, c, :], rhs=W_IN[:, c, b6 * 512:b6 * 512 + w], start=(c == 0), stop=(c == 7))
                if i == 0:
                    for (b32, c0, w) in ((0, 0, 512), (1, 512, 256)):
                        for c in range(8):
                            Sc.op("pe", "matmul", reads=["x32", "w32"], writes=[BK[b32]], out=bank(b32, w),
                                  lhsT=X32[:, c, :], rhs=W32[:, c, c0:c0 + w], start=(c == 0), stop=(c == 7))
                stage(2)
                PJ = ps[:, 1024:1024 + 3072]
                pjk = BK[2:8]

                def pk(c0, c1):
                    return [BK[2 + b] for b in range(c0 // 512, (c1 - 1) // 512 + 1)]
                rstd = sc1[:, 0:1]
                Sc.op("dve", "tensor_scalar", reads=["rstd"], writes=["rstdv"], out=sc1[:, 1:2], in0=rstd,
                      scalar1=-0.125, scalar2=None, op0=ALU.mult)
                Sc.op("dve", "tensor_scalar", reads=["rstd"], writes=["rstdv"], out=sc1[:, 2:3], in0=rstd,
                      scalar1=-1.0, scalar2=None, op0=ALU.mult)

                def rope(col0, nh, outs, srcflat=None, srckeys=None):
                    if srcflat is None:
                        srcflat = PJ[:, col0:col0 + nh * 64]
                    src = srcflat.rearrange("p (h d) -> p h d", h=nh)
                    pjk = srckeys if srckeys is not None else pk(col0, col0 + nh * 64)
                    A3 = ropA[:, 0:nh * 64].rearrange("p (h d) -> p h d", h=nh)
                    B3 = ropB[:, 0:nh * 64].rearrange("p (h d) -> p h d", h=nh)
                    Sc.op("dve", "scalar_tensor_tensor", reads=pjk + ["rstd", csk], writes=["ropA"], out=A3, in0=src,
                          scalar=rstd, in1=bc_mid(cs[:, 0:64], nh), op0=ALU.mult, op1=ALU.mult)
                    Sc.op("dve", "scalar_tensor_tensor", reads=pjk + ["rstd", csk], writes=["ropB"], out=B3[:, :, 0:32],
                          in0=src[:, :, 32:64], scalar=rstd, in1=bc_mid(cs[:, 64:96], nh), op0=ALU.mult, op1=ALU.mult)
                    Sc.op("dve", "scalar_tensor_tensor", reads=pjk + ["rstd", csk], writes=["ropB"], out=B3[:, :, 32:64],
                          in0=src[:, :, 0:32], scalar=rstd, in1=bc_mid(cs[:, 96:128], nh), op0=ALU.mult, op1=ALU.mult)
                    Sc.op("pool", "tensor_tensor", reads=["ropA", "ropB"], writes=["ropA"], out=ropA[:, 0:nh * 64],
                          in0=ropA[:, 0:nh * 64], in1=ropB[:, 0:nh * 64], op=ALU.add)
                    outs()

                def outs_a():
                    dst = qa_p[:].rearrange("p (j g d) -> p g j d", j=3, g=2)
                    srcq = ropA[:, 0:384].rearrange("p (g j d) -> p g j d", g=2, j=3)
                    Sc.op("pool", "tensor_copy", reads=["ropA"], writes=["qa_p"], out=dst, in_=srcq)
                    Sc.op("pool", "tensor_copy", reads=["ropA"], writes=["ka_r"], out=ka_r[:], in_=ropA[:, 384:512])

                rope(0, 8, outs_a)
                stage(3)
                Sc.op("act", "activation", reads=pk(512, 640) + ["rstd"], writes=["va_e%d" % r],
                      out=va_e[:, r, :, 0:64], in_=PJ[:, 512:640].rearrange("p (g d) -> p g d", g=2),
                      func=AF.Copy, scale=rstd)
                Sc.op("act", "activation", reads=pk(640, 896) + ["rstd"], writes=["qb_s"], out=qb_s[:], in_=PJ[:, 640:896],
                      func=AF.Copy, scale=rstd)
                Sc.op("act", "activation", reads=pk(896, 1152) + ["rstdv"], writes=["kb_n"], out=kb_n[:], in_=PJ[:, 896:1152],
                      func=AF.Copy, scale=sc1[:, 1:2])
                Sc.op("act", "activation", reads=pk(1152, 1408) + ["rstd"], writes=["vcache", "w32"], out=VC[:, i, :],
                      in_=PJ[:, 1152:1408], func=AF.Copy, scale=rstd)
                Sc.op("act", "activation", reads=pk(2176, 2560) + ["rstd"], writes=["vc_s"], out=vc_s[:], in_=PJ[:, 2176:2560],
                      func=AF.Copy, scale=rstd)
                Sc.op("act", "activation", reads=pk(2560, 2944) + ["rstdv"], writes=["silt"], out=silt[:], in_=PJ[:, 2560:2944],
                      func=AF.Exp, scale=sc1[:, 2:3])
                Sc.op("act", "activation", reads=pk(2560, 2944) + ["rstd"], writes=["sil"], out=sil[:], in_=PJ[:, 2560:2944],
                      func=AF.Copy, scale=rstd)

                stage(4)

                def outs_c():
                    if i == 0:
                        for h12 in range(12):
                            b32 = (0, 1, 7)[h12 // 4]
                            Sc.op("pe", "transpose", reads=["ropA", "cst"], writes=[BK[b32]],
                                  out=ps[0:64, b32 * 512 + (h12 % 4) * 128:b32 * 512 + (h12 % 4 + 1) * 128],
                                  in_=ropA[:, h12 * 64:(h12 + 1) * 64], identity=cst[:, C_ID:C_ID + 128])
                        for q3 in range(3):
                            b32 = (0, 1, 7)[q3]
                            Sc.op("dve", "tensor_copy", reads=[BK[b32]], writes=["qk32"],
                                  out=QK32[:, q3 * 4:(q3 + 1) * 4, :].rearrange("p h t -> p (h t)"),
                                  in_=ps[0:64, b32 * 512:(b32 + 1) * 512])
                    Sc.op("pool", "tensor_copy", reads=["ropA"], writes=["qc_r"], out=qc_r[:], in_=ropA[:, 0:384])
                    Sc.op("pool", "tensor_copy", reads=["ropA"], writes=["kc_r"], out=kc_r[:], in_=ropA[:, 384:768])
                    Sc.op("pool", "tensor_tensor", reads=["ropA", "cst"], writes=["qcd"],
                          out=qcd[:].rearrange("p (h d) -> p h d", h=6),
                          in0=ropA[:, 0:384].rearrange("p (h d) -> p h d", h=6),
                          in1=bc_last(cst[:, C_QD:C_QD + 6], 64), op=ALU.mult)
                    Sc.op("pool", "tensor_tensor", reads=["ropA", "cst"], writes=["kdk"],
                          out=kdk[:].rearrange("p (h d) -> p h d", h=6),
                          in0=ropA[:, 384:768].rearrange("p (h d) -> p h d", h=6),
                          in1=bc_last(cst[:, C_KD:C_KD + 6], 64), op=ALU.mult)

                if i == 0:
                    rope(1408, 12, outs_c, srcflat=ps[:, 0:768], srckeys=[BK[0], BK[1]])
                else:
                    rope(1408, 12, outs_c)
                Sc.op("dve", "tensor_scalar", reads=["silt"], writes=["silt"], out=silt[:], in0=silt[:], scalar1=1.0,
                      scalar2=None, op0=ALU.add)
                Sc.op("dve", "reciprocal", reads=["silt"], writes=["silt"], out=silt[:], in_=silt[:])
                Sc.op("pool", "tensor_tensor", reads=["sil", "silt"], writes=["sil"], out=sil[:], in0=sil[:], in1=silt[:],
                      op=ALU.mult)
                stage(5)

                for j in range(3):
                    transpose_bf(qa_p[:, j * 128:(j + 1) * 128], 128, 0, j, ["qa_p"])
                transpose_bf(ka_r[:], 128, 0, 3, ["ka_r"])
                for j in range(2):
                    transpose_bf(qb_s[:, j * 128:(j + 1) * 128], 128, 0, 4 + j, ["qb_s"])
                for j in range(2):
                    transpose_bf(kb_n[:, j * 128:(j + 1) * 128], 128, 0, 6 + j, ["kb_n"])
                b0 = bankbf(0)
                Sc.op("dve", "tensor_copy", reads=[BK[0]], writes=["qaT"], out=qaT[:].rearrange("p j t -> p (j t)"),
                      in_=b0[:, 0:384])
                Sc.op("dve", "tensor_copy", reads=[BK[0]], writes=["kaT%d" % r], out=kaT[:, r, :], in_=b0[:, 384:512])
                for par in range(2):
                    Sc.op("dve", "tensor_copy", reads=[BK[0]], writes=["qbTp"],
                          out=qbTp[par * 64:(par + 1) * 64, :, :].rearrange("p (j q) t -> p j q t", q=2)[:, :, par, :],
                          in_=b0[par * 64:(par + 1) * 64, 512:768].rearrange("p (j t) -> p j t", j=2))
                Sc.op("dve", "tensor_copy", reads=[BK[0]], writes=["kbT", "x32", "qk32"], out=KBT[:, :, i * 128:(i + 1) * 128],
                      in_=b0[:, 768:1024].rearrange("p (j t) -> p j t", j=2))
                for j in range(3):
                    transpose_bf(qc_r[:, j * 128:(j + 1) * 128], 128, 1, j, ["qc_r"])
                for j in range(3):
                    transpose_bf(kc_r[:, j * 128:(j + 1) * 128], 128, 1, 3 + j, ["kc_r"])
                b1 = bankbf(1)
                Sc.op("dve", "tensor_copy", reads=[BK[1]], writes=["qcT"], out=qcT[:].rearrange("p j t -> p (j t)"),
                      in_=b1[:, 0:384])
                Sc.op("dve", "tensor_copy", reads=[BK[1]], writes=["kcT"], out=kcT[:].rearrange("p j t -> p (j t)"),
                      in_=b1[:, 384:768])
                for h in range(6):
                    transpose_bf(qcd[:, h * 64:(h + 1) * 64], 64, 0, h, ["qcd"])
                Sc.op("dve", "tensor_copy", reads=[BK[0]], writes=["qcdT"], out=qcdT[:].rearrange("p j t -> p (j t)"),
                      in_=bankbf(0)[0:64, 0:768])

                stage(6)

                def swa_gen():
                    kbs = [1] if i == 0 else [0, 1]
                    for kbi in kbs:
                        rr = r if kbi == 1 else 1 - r
                        for g in range(2):
                            b = 2 + kbi * 2 + g
                            Sc.op("pe", "matmul", reads=["kaT%d" % rr, "qaT"], writes=[BK[b]], out=bank(b, 384),
                                  lhsT=kaT[g * 64:(g + 1) * 64, rr, :], rhs=qaT[g * 64:(g + 1) * 64, :, :],
                                  start=True, stop=True)
                        yield
                    for kbi in kbs:
                        b = 2 + kbi * 2
                        Sc.op("act", "activation", reads=[BK[b], BK[b + 1]], writes=["PT%d" % kbi],
                              out=PT[:, kbi, :].rearrange("p (g n) -> p g n", g=2),
                              in_=ps[:, b * 512:(b + 2) * 512].rearrange("p (g n) -> p g n", g=2)[:, :, 0:384],
                              func=AF.Exp, scale=0.125)
                        msk = mcurb if kbi == 1 else mprevb
                        Sc.op("pool", "tensor_tensor", reads=["PT%d" % kbi, "mcurb", "mprevb"], writes=["PT%d" % kbi],
                              out=PT[:, kbi, :].rearrange("p (h t) -> p h t", h=6),
                              in0=PT[:, kbi, :].rearrange("p (h t) -> p h t", h=6), in1=bc_mid(msk[:], 6), op=ALU.mult)
                        yield
                    for h in range(6):
                        g, j = divmod(h, 3)
                        for n_, kbi in enumerate(kbs):
                            rr = r if kbi == 1 else 1 - r
                            Sc.op("pe", "matmul", reads=["PT%d" % kbi, "va_e%d" % rr], writes=[BK[6]],
                                  out=bank(6, 66, h * 66), lhsT=PT[:, kbi, (g * 3 + j) * 128:(g * 3 + j + 1) * 128],
                                  rhs=va_e[:, rr, g, :], start=(n_ == 0), stop=(n_ == len(kbs) - 1))
                    yield
                    o6 = bank(6, 396).rearrange("p (h e) -> p h e", h=6)
                    Sc.op("dve", "tensor_tensor", reads=[BK[6], "esink"], writes=["den"], out=den[:], in0=o6[:, :, 64],
                          in1=es, op=ALU.add)
                    Sc.op("dve", "reciprocal", reads=["den"], writes=["den"], out=den[:], in_=den[:])
                    Sc.op("dve", "tensor_tensor", reads=[BK[6], "den"], writes=["oa"],
                          out=oa[:].rearrange("p (h d) -> p h d", h=6), in0=o6[:, :, 0:64], in1=bc_last(den[:], 64),
                          op=ALU.mult)
                    yield
                    rstd_sq([oa[:]], 384, sc1[:, 3:4], ["oa"], "rstd_a", 2)
                    yield
                    Sc.op("dve", "tensor_scalar", reads=["oa", "rstd_a"], writes=["mixed"], out=mixed[:, 0:384], in0=oa[:],
                          scalar1=sc1[:, 3:4], scalar2=None, op0=ALU.mult)

                def ret_gen():
                    for h in range(6):
                        b = h % 2
                        pb = (h % 2) * 64
                        if i == 0:
                            Sc.op("pe", "matmul", reads=["qk32"], writes=[BK[b]], out=bank(b, 128, (h // 2) * 128),
                                  lhsT=QK32[:, 6 + h, :], rhs=QK32[:, h, :], start=True, stop=True)
                        else:
                            Sc.op("pe", "matmul", reads=["kcT", "qcT"], writes=[BK[b]], out=bank(b, 128, (h // 2) * 128),
                                  lhsT=kcT[pb:pb + 64, h // 2, :], rhs=qcT[pb:pb + 64, h // 2, :], start=True, stop=True)
                    yield
                    Sc.op("dve", "tensor_tensor", reads=[BK[0], BK[1], "cst"], writes=["intraT"],
                          out=intraT[:].rearrange("p (j par t) -> p par j t", j=3, par=2),
                          in0=ps[:, 0:1024].rearrange("p (par n) -> p par n", par=2)[:, :, 0:384].rearrange(
                              "p par (j t) -> p par j t", j=3),
                          in1=cst[:, C_DD:C_DD + 768].rearrange("p (j par t) -> p par j t", j=3, par=2), op=ALU.mult)
                    yield
                    for h in range(6):
                        Sc.op("pe", "matmul", reads=["intraT", "vc_s"], writes=[BK[7]], out=bank(7, 64, h * 64),
                              lhsT=intraT[:, h * 128:(h + 1) * 128], rhs=vc_s[:, h * 64:(h + 1) * 64],
                              start=True, stop=(i == 0))
                        if i > 0:
                            Sc.op("pe", "matmul", reads=["qcdT", "stateb"], writes=[BK[7]], out=bank(7, 64, h * 64),
                                  lhsT=qcdT[:, h, :], rhs=stateb[:, h * 64:(h + 1) * 64], start=False, stop=True)
                    for h in range(6):
                        Sc.op("pe", "matmul", reads=["kdk", "vc_s"], writes=[BK[0]], out=ps[0:64, h * 64:(h + 1) * 64],
                              lhsT=kdk[:, h * 64:(h + 1) * 64], rhs=vc_s[:, h * 64:(h + 1) * 64], start=True, stop=True)
                    yield
                    o7 = bank(7, 384).rearrange("p (h d) -> p h d", h=6)
                    Sc.op("dve", "tensor_reduce", reads=[BK[7]], writes=["gmv"], out=gmv[:, 0:6], in_=o7,
                          axis=mybir.AxisListType.X, op=ALU.add)
                    Sc.op("act", "activation", reads=[BK[7]], writes=["oc"], out=oc[:], in_=bank(7, 384), func=AF.Square)
                    yield
                    Sc.op("dve", "tensor_reduce", reads=["oc"], writes=["gst"], out=gst[:, 0:6],
                          in_=oc[:].rearrange("p (h d) -> p h d", h=6), axis=mybir.AxisListType.X, op=ALU.add)
                    Sc.op("dve", "tensor_scalar", reads=["gmv"], writes=["gmv"], out=gmv[:, 0:6], in0=gmv[:, 0:6],
                          scalar1=1.0 / 64, scalar2=None, op0=ALU.mult)
                    Sc.op("dve", "tensor_tensor", reads=["gmv"], writes=["gst2"], out=gst[:, 6:12], in0=gmv[:, 0:6],
                          in1=gmv[:, 0:6], op=ALU.mult)
                    Sc.op("dve", "scalar_tensor_tensor", reads=["gst", "gst2"], writes=["gvar"], out=gmv[:, 6:12], in0=gst[:, 0:6],
                          scalar=1.0 / 64, in1=gst[:, 6:12], op0=ALU.mult, op1=ALU.subtract)
                    Sc.op("act", "activation", reads=["gvar"], writes=["grs"], out=grs[:], in_=gmv[:, 6:12], func=AF.Ln,
                          bias=EPS, scale=1.0)
                    Sc.op("act", "activation", reads=["grs"], writes=["grs"], out=grs[:], in_=grs[:], func=AF.Exp,
                          scale=-0.5)
                    yield
                    for h in range(6):
                        cd = math.exp(128.0 * LOG_GAMMA[h])
                        Sc.op("dve", "scalar_tensor_tensor", reads=[BK[0], "state"], writes=["state"],
                              out=state[:, h * 64:(h + 1) * 64], in0=state[:, h * 64:(h + 1) * 64], scalar=cd,
                              in1=ps[0:64, h * 64:(h + 1) * 64], op0=ALU.mult, op1=ALU.add)
                    Sc.op("pool", "tensor_copy", reads=["state"], writes=["stateb"], out=stateb[:], in_=state[:])
                    yield
                    oc3 = oc[:].rearrange("p (h d) -> p h d", h=6)
                    Sc.op("dve", "tensor_tensor", reads=[BK[7], "gmv", "gst"], writes=["oc"], out=oc3,
                          in0=bank(7, 384).rearrange("p (h d) -> p h d", h=6), in1=bc_last(gmv[:, 0:6], 64),
                          op=ALU.subtract)
                    Sc.op("pool", "tensor_tensor", reads=["oc", "grs"], writes=["oc"], out=oc3, in0=oc3,
                          in1=bc_last(grs[:], 64), op=ALU.mult)
                    Sc.op("pool", "tensor_tensor", reads=["oc", "sil"], writes=["mixed"], out=mixed[:, 640:1024], in0=oc[:],
                          in1=sil[:], op=ALU.mult)

                gens = [swa_gen(), ret_gen()]
                while gens:
                    for g_ in list(gens):
                        try:
                            next(g_)
                        except StopIteration:
                            gens.remove(g_)

                stage(8)
                if DBG.get('skip_sb'):
                    steps = [[i]]
                else:
                    steps = [[i]]
                    jj = i - 1
                    while jj >= 0:
                        if jj >= 1:
                            steps.append([jj, jj - 1]); jj -= 2
                        else:
                            steps.append([jj]); jj -= 1
                c_state = [False]

                def sb_cfg(n_):
                    js = steps[n_]
                    s_ = n_ % 2
                    ab = [2 * (n_ % 3), 2 * (n_ % 3) + 1]
                    return js, s_, len(js), ab, [BK[ab[u]] for u in range(len(js))], (js[0] == i), "Lt%d" % s_, "Wt%d" % s_

                def sb_z(n_):
                    js, s, nk, ab, abk, diag, Lk, Wk = sb_cfg(n_)
                    for u, j in enumerate(js):
                        for h in range(4):
                            Sc.op("pe", "matmul", reads=["kbT", "qbTp"], writes=[BK[ab[u]]], out=bank(ab[u], 128, h * 128),
                                  lhsT=KBT[:, h // 2, j * 128:(j + 1) * 128], rhs=qbTp[:, h, :],
                                  start=(h == 0), stop=False, skip_group_check=True)

                def sb_el(n_):
                    js, s, nk, ab, abk, diag, Lk, Wk = sb_cfg(n_)
                    Aall = ps[:, ab[0] * 512:ab[0] * 512 + 512 * nk]
                    Sc.op("act", "activation", reads=abk, writes=["Et"], out=Et[:, 0:512 * nk], in_=Aall, func=AF.Exp, scale=-1.0)
                    Sc.op("act", "activation", reads=["Et"], writes=[Lk], out=Lt[:, s, 0:512 * nk], in_=Et[:, 0:512 * nk],
                          func=AF.Ln, bias=1.0, scale=1.0)
                    if diag:
                        Sc.op("pool", "tensor_tensor", reads=[Lk, "mstrb"], writes=[Lk],
                              out=Lt[:, s, 0:512].rearrange("p (h t) -> p h t", h=4),
                              in0=Lt[:, s, 0:512].rearrange("p (h t) -> p h t", h=4), in1=bc_mid(mstrb[:], 4), op=ALU.mult)

                def sb_cum(n_):
                    js, s, nk, ab, abk, diag, Lk, Wk = sb_cfg(n_)
                    for u in range(nk):
                        Sc.op("pe", "matmul", reads=[Lk, "uinclb"], writes=[BK[ab[u]]], out=bank(ab[u]), lhsT=uinclb[:],
                              rhs=Lt[:, s, u * 512:(u + 1) * 512], start=False, stop=(u == 0), skip_group_check=True)
                    if nk == 2:
                        Sc.op("pe", "matmul", reads=[Lk, "onesq"], writes=[BK[ab[1]]], out=bank(ab[1]), lhsT=onesq[:],
                              rhs=Lt[:, s, 0:512], start=False, stop=True, skip_group_check=True)

                def sb_rest(n_):
                    js, s, nk, ab, abk, diag, Lk, Wk = sb_cfg(n_)
                    pbk = 6
                    Aall = ps[:, ab[0] * 512:ab[0] * 512 + 512 * nk]
                    if not diag:
                        Sc.op("act", "activation", reads=[BK[7]], writes=["gt%d" % s], out=gt[:, s, :],
                              in_=bank(7, 8).rearrange("p (h k) -> p h k", h=4)[:, :, 0], func=AF.Exp, scale=-1.0)
                    Sc.op("act", "activation", reads=abk, writes=[Wk], out=Wt[:, s, 0:512 * nk], in_=Aall, func=AF.Exp, scale=-1.0)
                    if diag:
                        Sc.op("pool", "tensor_tensor", reads=[Wk, "mstrb"], writes=[Wk],
                              out=Wt[:, s, 0:512].rearrange("p (h t) -> p h t", h=4),
                              in0=Wt[:, s, 0:512].rearrange("p (h t) -> p h t", h=4), in1=bc_mid(mstrb[:], 4), op=ALU.mult)
                    for h in range(4):
                        for u, j in enumerate(js):
                            Sc.op("pe", "matmul", reads=[Wk, "vcache"], writes=[BK[pbk]], out=bank(pbk, 64, h * 64),
                                  lhsT=Wt[:, s, u * 512 + h * 128:u * 512 + (h + 1) * 128], rhs=VC[:, j, h * 64:(h + 1) * 64],
                                  start=(u == 0), stop=(u == nk - 1))
                    if diag:
                        Sc.op("dve", "tensor_copy", reads=[BK[pbk]], writes=["Oacc"], out=Oacc[:], in_=bank(pbk, 256))
                    else:
                        for h in range(4):
                            Sc.op("dve", "scalar_tensor_tensor", reads=[BK[pbk], "gt%d" % s, "Oacc"], writes=["Oacc"],
                                  out=Oacc[:, h * 64:(h + 1) * 64], in0=bank(pbk, 64, h * 64), scalar=gt[:, s, h:h + 1],
                                  in1=Oacc[:, h * 64:(h + 1) * 64], op0=ALU.mult, op1=ALU.add)

                def sb_colsum(n_):
                    js, s, nk, ab, abk, diag, Lk, Wk = sb_cfg(n_)
                    if js[-1] > 0:
                        for u in range(nk):
                            for h in range(4):
                                Sc.op("pe", "matmul", reads=[Lk, "onesb"], writes=[BK[7]], out=bank(7, 2, h * 2),
                                      lhsT=Lt[:, s, u * 512 + h * 128:u * 512 + (h + 1) * 128], rhs=onesb[:],
                                      start=(not c_state[0]), stop=False, skip_group_check=True)
                                c_state[0] = True

                NS = len(steps)
                sb_z(0)
                if NS > 1:
                    sb_z(1)
                sb_el(0)
                for n_ in range(NS):
                    sb_cum(n_)
                    if n_ + 2 < NS:
                        sb_z(n_ + 2)
                    if n_ + 1 < NS:
                        sb_el(n_ + 1)
                    sb_rest(n_)
                    sb_colsum(n_)
                rstd_sq([Oacc[:]], 256, sc1[:, 4:5], ["Oacc"], "rstd_b", 4)
                Sc.op("dve", "tensor_scalar", reads=["Oacc", "rstd_b"], writes=["mixed"], out=mixed[:, 384:640],
                      in0=Oacc[:], scalar1=sc1[:, 4:5], scalar2=None, op0=ALU.mult)

                stage(9)
                for c in range(8):
                    transpose_bf(mixed[:, c * 128:(c + 1) * 128], 128, 4, c, ["mixed"])
                Sc.op("dve", "tensor_tensor", reads=[BK[4], "gcols"], writes=["mixedT"], out=mixedT[:],
                      in0=bankbf(4).rearrange("p (c t) -> p c t", c=8), in1=bc_last(gcols[:, gb + 16:gb + 24], 128),
                      op=ALU.mult)
                if i + 1 < nblk:
                    stats_transpose(i + 1, gb + 0)
                for half in range(2):
                    for c in range(8):
                        Sc.op("pe", "matmul", reads=["mixedT", "w_out"], writes=[BK[5 + half]], out=bank(5 + half),
                              lhsT=mixedT[:, c, :], rhs=W_OUT[:, c, half * 512:(half + 1) * 512],
                              start=(c == 0), stop=(c == 7))
                rstd_sq([bank(5), bank(6)], 1024, sc1[:, 5:6], [BK[5], BK[6]], "rstd_y", 6)
                residual_out([5, 6], sc1[:, 5:6], "rstd_y", i)

            stage(10)
            prs = []
            for c in range(8):
                for hh in range(2):
                    prs.append((W_UP[:, c, hh * 2048:(hh + 1) * 2048],
                                w_up[l, c * 128:(c + 1) * 128, hh * 2048:(hh + 1) * 2048]))
            for c4 in range(4):
                prs.append((W_DN[:, c4 * 8:(c4 + 1) * 8, :],
                            w_down[l, c4 * 1024:(c4 + 1) * 1024, :].rearrange("(c p) n -> p c n", p=128)))
            Sc.dma_multi("pool", "wld", prs, writes=BIGKEYS)
            Sc.dma("sp", "gbc", gbc[:], gbc_in[l * 2 + 1], writes=["gbc"])
            nblk_f = nblk if not DBG.get('skip_ffn') else 0
            if nblk_f:
                load_x(y_out, 0, False, t0row=2 * l + 1)
            for i in range(nblk_f):
                if i + 1 < nblk_f:
                    load_x(y_out, i + 1, False)
                if i == 0:
                    stats_transpose(i, gb + 8, want_s1=True)

                def up(fb):
                    b = 2 + fb % 4
                    for f4 in range(4):
                        fc = fb * 4 + f4
                        for c in range(8):
                            Sc.op("pe", "matmul", reads=["xgT", "w_up"], writes=[BK[b]], out=bank(b, 128, f4 * 128),
                                  lhsT=W_UP[:, c, fc * 128:(fc + 1) * 128], rhs=xgT[:, c, :],
                                  start=(c == 0), stop=(c == 7))
                    rs = fb % 2
                    Sc.op("dve", "tensor_scalar", reads=[BK[b]], writes=["rtmp%d" % rs], out=rtmp[:, rs, :], in0=bank(b),
                          scalar1=0.0, scalar2=None, op0=ALU.max)
                    Sc.op("pool", "tensor_tensor", reads=["rtmp%d" % rs], writes=["hT%d" % fb],
                          out=hT[:, fb * 4:(fb + 1) * 4, :].rearrange("p a t -> p (a t)"), in0=rtmp[:, rs, :],
                          in1=rtmp[:, rs, :], op=ALU.mult)

                def down(fb):
                    for f4 in range(4):
                        fc = fb * 4 + f4
                        for half in range(2):
                            Sc.op("pe", "matmul", reads=["hT%d" % fb, "w_dn"], writes=[BK[6 + half]], out=bank(6 + half),
                                  lhsT=hT[:, fc, :], rhs=W_DN[:, fc, half * 512:(half + 1) * 512],
                                  start=(fc == 0), stop=(fc == 31))

                up(0)
                for fb in range(8):
                    if fb + 1 < 8:
                        up(fb + 1)
                    elif i + 1 < nblk_f:
                        stats_transpose(i + 1, gb + 8, want_s1=True)
                    down(fb)
                s1k = "s1_%d" % (i % 2)
                s1ap = sc1[:, 6 + i % 2:7 + i % 2]
                rstd_sq([bank(6), bank(7)], 1024, sc1[:, 5:6], [BK[6], BK[7]], "rstd_y", 8,
                        scale_ap=ssq[:, 12 + i % 2:13 + i % 2], scale_key="s1q_%d" % (i % 2))
                Sc.op("dve", "tensor_tensor", reads=["rstd_y", s1k], writes=["rstd_y"], out=sc1[:, 5:6], in0=sc1[:, 5:6],
                      in1=s1ap, op=ALU.mult)
                residual_out([6, 7], sc1[:, 5:6], "rstd_y", i)

        if nlayers > 0 and not Sc.stopped:
            Sc.dma("sp", "t0fin", ytmp[0:1, :], t0tab[2 * nlayers:2 * nlayers + 1, :], reads=["t0tab%d" % (2 * nlayers)],
                   writes=["ytmp"])
            Sc.dma("sp", "t0fin", y_out[0:1, :], ytmp[0:1, :], reads=["ytmp", "xsrc0"], writes=["xsrc0"])
        Sc.wait_all("sp", ["xsrc%d" % i for i in range(nblk)])
        Sc.emit()
    return nc


def prep_inputs(x, positions, w_in, w_out, sinks, branch_gain, w_up, w_down,
                norm_mix_pre, norm_mix_post, norm_mlp_pre, norm_mlp_post):
    f = lambda a: np.ascontiguousarray(np.asarray(a, dtype=np.float32))
    cols = np.zeros((128, DEPTH, 3, 8), np.float32)
    for l in range(DEPTH):
        for k, g in enumerate((norm_mix_pre, norm_mlp_pre, branch_gain)):
            cols[:, l, k, :] = np.asarray(g, np.float32)[l].reshape(8, 128).T
    gb = np.zeros((DEPTH, 2, 128, D), np.float32)
    for l in range(DEPTH):
        gb[l, 0] = np.broadcast_to(np.asarray(norm_mix_post, np.float32)[l][None, :], (128, D))
        gb[l, 1] = np.broadcast_to(np.asarray(norm_mlp_post, np.float32)[l][None, :], (128, D))
    gr = np.zeros((DEPTH, 5, D), np.float32)
    for l in range(DEPTH):
        for k, g in enumerate((norm_mix_pre, branch_gain, norm_mix_post, norm_mlp_pre, norm_mlp_post)):
            gr[l, k] = np.asarray(g, np.float32)[l]
    shared = {
        "grows": np.ascontiguousarray(gr.reshape(DEPTH, 5 * D)),
        "w_in": f(w_in), "w_out": f(w_out), "w_up": f(w_up), "w_down": f(w_down),
        "consts": make_consts(),
        "gcols": np.ascontiguousarray(cols.reshape(128, DEPTH * 24)),
        "sinksb": np.ascontiguousarray(np.broadcast_to(np.asarray(sinks, np.float32).reshape(1, DEPTH * 6), (128, DEPTH * 6))),
        "gbc": np.ascontiguousarray(gb.reshape(DEPTH * 2, 128, D)),
        "pos": np.ascontiguousarray(np.asarray(positions, dtype=np.int32).reshape(64, 128)),
    }
    return shared


_NC_CACHE = {}


def kernel(x, positions, w_in, w_out, sinks, branch_gain, w_up, w_down,
           norm_mix_pre, norm_mix_post, norm_mlp_pre, norm_mlp_post):
    x = np.asarray(x, dtype=np.float32)
    shared = prep_inputs(x, positions, w_in, w_out, sinks, branch_gain, w_up, w_down,
                         norm_mix_pre, norm_mix_post, norm_mlp_pre, norm_mlp_post)
    if "nc" not in _NC_CACHE:
        _NC_CACHE["nc"] = build_nc()
    nc = _NC_CACHE["nc"]
    in_maps = []
    for b in range(8):
        m = dict(shared)
        m["x"] = np.ascontiguousarray(x[b])
        in_maps.append(m)
    res = run_bass_kernel_spmd(nc, in_maps, core_ids=list(range(8)))
    return np.stack([np.asarray(r["y"], dtype=np.float32) for r in res.results], axis=0)
```
